# Optimizing a Trainium2 kernel written in Bass

```python
import math
import jax, jax.numpy as jnp
from jax import lax
import numpy as np

D_MODEL = 1024
BATCH = 8
SEQ = 8192
DEPTH = 1
DEC_BATCH = 1
DEC_SEQ = 16384
PAST_LEN = 128

FNET_GROUPS = 4
FNET_GROUP_DIM = 128
FNET_WIDTH = FNET_GROUPS * FNET_GROUP_DIM
SSM_GROUP_DIM = 16
SSM_GROUPS = 32
SSM_WIDTH = SSM_GROUPS * SSM_GROUP_DIM
SSM_STATE = 64
N_DIR = 2
DT_MIN = 1e-3
DT_MAX = 1e-1
IN_WIDTH = FNET_WIDTH + SSM_WIDTH + 2 * D_MODEL
FFN_HIDDEN = int(math.ceil(8 * D_MODEL / 3 / 256) * 256)
EPS = 1e-6

kernel_name = "hybrid_fnet_s5_gated_encoder"


def rms_norm(x, g):
    xf = x.astype(jnp.float32)
    y = xf * lax.rsqrt(jnp.mean(xf * xf, axis=-1, keepdims=True) + EPS)
    return (y * g.astype(jnp.float32)).astype(x.dtype)


def fourier_mix(u):
    b, s, _ = u.shape
    ug = u.astype(jnp.float32).reshape(b, s, FNET_GROUPS, FNET_GROUP_DIM)
    f = jnp.fft.fftn(ug, axes=(1, 3), norm="ortho").real
    return f.reshape(b, s, FNET_WIDTH).astype(u.dtype)


def _scan_op(left, right):
    a_l, b_l = left
    a_r, b_r = right
    return a_r * a_l, a_r * b_l + b_r


def ssm_direction(u, lam_re, lam_im, log_dt, b_re, b_im, c_re, c_im, reverse):
    f32 = jnp.float32
    lam = lax.complex(lam_re.astype(f32), lam_im.astype(f32))
    dt = jnp.exp(log_dt.astype(f32))[:, None]
    lam_bar = jnp.exp(lam * dt)
    b = lax.complex(b_re.astype(f32), b_im.astype(f32))
    b_bar = ((lam_bar - 1.0) / lam)[:, :, None] * b
    bu = jnp.einsum('sgh,gph->sgp', u.astype(jnp.complex64), b_bar)
    a = jnp.broadcast_to(lam_bar, bu.shape)
    _, states = lax.associative_scan(_scan_op, (a, bu), reverse=reverse, axis=0)
    c = lax.complex(c_re.astype(f32), c_im.astype(f32))
    return jnp.einsum('sgp,ghp->sgh', states, c).real


def ssm_sequence(u, lam_re, lam_im, log_dt, b_re, b_im, c_re, c_im, d_skip):
    s = u.shape[0]
    ug = u.astype(jnp.float32).reshape(s, SSM_GROUPS, SSM_GROUP_DIM)
    y_fwd = ssm_direction(ug, lam_re[0], lam_im[0], log_dt[0], b_re[0], b_im[0], c_re[0], c_im[0], False)
    y_bwd = ssm_direction(ug, lam_re[1], lam_im[1], log_dt[1], b_re[1], b_im[1], c_re[1], c_im[1], True)
    y = y_fwd + y_bwd + d_skip.astype(jnp.float32).reshape(SSM_GROUPS, SSM_GROUP_DIM) * ug
    return y.reshape(s, SSM_WIDTH).astype(u.dtype)


def mixer(h, w_in, w_fnet_out, lam_re, lam_im, log_dt, b_re, b_im, c_re, c_im, d_skip,
          w_glu_val, w_glu_gate, w_out):
    proj = jnp.einsum('bsd,de->bse', h, w_in)
    o1 = FNET_WIDTH
    o2 = o1 + SSM_WIDTH
    o3 = o2 + D_MODEL
    u_f, u_s, g_a, g_b = proj[..., :o1], proj[..., o1:o2], proj[..., o2:o3], proj[..., o3:]
    br_a = jnp.einsum('bsf,fd->bsd', fourier_mix(u_f), w_fnet_out)
    y_s = lax.map(lambda u: ssm_sequence(u, lam_re, lam_im, log_dt, b_re, b_im, c_re, c_im, d_skip), u_s)
    z = jax.nn.gelu(y_s)
    br_b = jnp.einsum('bsf,fd->bsd', z, w_glu_val) * jax.nn.sigmoid(jnp.einsum('bsf,fd->bsd', z, w_glu_gate))
    merged = jax.nn.sigmoid(g_a) * br_a + jax.nn.sigmoid(g_b) * br_b
    return jnp.einsum('bsd,de->bse', merged, w_out)


def swiglu(h, w_gate, w_up, w_down):
    g = jnp.einsum('bsd,df->bsf', h, w_gate)
    u = jnp.einsum('bsd,df->bsf', h, w_up)
    return jnp.einsum('bsf,fd->bsd', jax.nn.silu(g) * u, w_down)


def encoder_stack(x, norm_mix_pre, norm_mix_post, norm_ffn_pre, norm_ffn_post, w_in, w_fnet_out,
                  lam_re, lam_im, log_dt, b_re, b_im, c_re, c_im, d_skip, w_glu_val, w_glu_gate,
                  w_out, w_ffn_gate, w_ffn_up, w_ffn_down):
    for l in range(DEPTH):
        h = rms_norm(x, norm_mix_pre[l])
        m = mixer(h, w_in[l], w_fnet_out[l], lam_re[l], lam_im[l], log_dt[l], b_re[l], b_im[l],
                  c_re[l], c_im[l], d_skip[l], w_glu_val[l], w_glu_gate[l], w_out[l])
        x = x + rms_norm(m, norm_mix_post[l])
        h = rms_norm(x, norm_ffn_pre[l])
        f = swiglu(h, w_ffn_gate[l], w_ffn_up[l], w_ffn_down[l])
        x = x + rms_norm(f, norm_ffn_post[l])
    return x


def setup_inputs(seed: int = 0) -> dict:
    key = jax.random.key(seed)
    ks = jax.random.split(key, 24)
    f32 = jnp.float32
    L = DEPTH

    def nrm(k, shape, fan_in):
        return jax.random.normal(k, shape, f32) * (fan_in ** -0.5)

    def gain(k):
        return 1.0 + 0.02 * jax.random.normal(k, (L, D_MODEL), f32)

    n_idx = jnp.arange(SSM_STATE, dtype=f32)
    lam_re = -0.5 + 0.01 * jax.random.normal(ks[6], (L, N_DIR, SSM_GROUPS, SSM_STATE), f32)
    lam_im = math.pi * n_idx + 0.01 * jax.random.normal(ks[7], (L, N_DIR, SSM_GROUPS, SSM_STATE), f32)
    log_dt = jax.random.uniform(ks[8], (L, N_DIR, SSM_GROUPS), f32,
                                minval=math.log(DT_MIN), maxval=math.log(DT_MAX))
    return {
        "x_prompt": jax.random.normal(ks[0], (BATCH, SEQ, D_MODEL), f32),
        "x_sample": jax.random.normal(ks[1], (DEC_BATCH, DEC_SEQ, D_MODEL), f32),
        "norm_mix_pre": gain(ks[2]),
        "norm_mix_post": gain(ks[3]),
        "norm_ffn_pre": gain(ks[4]),
        "norm_ffn_post": gain(ks[5]),
        "w_in": nrm(ks[9], (L, D_MODEL, IN_WIDTH), D_MODEL),
        "w_fnet_out": nrm(ks[10], (L, FNET_WIDTH, D_MODEL), FNET_WIDTH),
        "lam_re": lam_re,
        "lam_im": lam_im,
        "log_dt": log_dt,
        "b_re": nrm(ks[11], (L, N_DIR, SSM_GROUPS, SSM_STATE, SSM_GROUP_DIM), 2 * SSM_GROUP_DIM),
        "b_im": nrm(ks[12], (L, N_DIR, SSM_GROUPS, SSM_STATE, SSM_GROUP_DIM), 2 * SSM_GROUP_DIM),
        "c_re": nrm(ks[13], (L, N_DIR, SSM_GROUPS, SSM_GROUP_DIM, SSM_STATE), 2 * SSM_STATE),
        "c_im": nrm(ks[14], (L, N_DIR, SSM_GROUPS, SSM_GROUP_DIM, SSM_STATE), 2 * SSM_STATE),
        "d_skip": 1.0 + 0.1 * jax.random.normal(ks[15], (L, SSM_WIDTH), f32),
        "w_glu_val": nrm(ks[16], (L, SSM_WIDTH, D_MODEL), SSM_WIDTH),
        "w_glu_gate": nrm(ks[17], (L, SSM_WIDTH, D_MODEL), SSM_WIDTH),
        "w_out": nrm(ks[18], (L, D_MODEL, D_MODEL), D_MODEL),
        "w_ffn_gate": nrm(ks[19], (L, D_MODEL, FFN_HIDDEN), D_MODEL),
        "w_ffn_up": nrm(ks[20], (L, D_MODEL, FFN_HIDDEN), D_MODEL),
        "w_ffn_down": nrm(ks[21], (L, FFN_HIDDEN, D_MODEL), FFN_HIDDEN),
    }


def reference(x_prompt, x_sample, norm_mix_pre, norm_mix_post, norm_ffn_pre, norm_ffn_post, w_in,
              w_fnet_out, lam_re, lam_im, log_dt, b_re, b_im, c_re, c_im, d_skip, w_glu_val,
              w_glu_gate, w_out, w_ffn_gate, w_ffn_up, w_ffn_down):
    y_prompt = encoder_stack(x_prompt, norm_mix_pre, norm_mix_post, norm_ffn_pre, norm_ffn_post, w_in,
                             w_fnet_out, lam_re, lam_im, log_dt, b_re, b_im, c_re, c_im, d_skip,
                             w_glu_val, w_glu_gate, w_out, w_ffn_gate, w_ffn_up, w_ffn_down)
    y_sample = encoder_stack(x_sample, norm_mix_pre, norm_mix_post, norm_ffn_pre, norm_ffn_post, w_in,
                             w_fnet_out, lam_re, lam_im, log_dt, b_re, b_im, c_re, c_im, d_skip,
                             w_glu_val, w_glu_gate, w_out, w_ffn_gate, w_ffn_up, w_ffn_down)
    return (y_prompt, y_sample)
```

```python
import math
from contextlib import ExitStack

import numpy as np
import ml_dtypes

import concourse.bass as bass
import concourse.mybir as mybir
from concourse.bass_utils import run_bass_kernel_spmd

F32 = mybir.dt.float32
BF16 = mybir.dt.bfloat16
AF = mybir.ActivationFunctionType
ALU = mybir.AluOpType

D = 1024
SP = 8192
SS = 16384
OWN = 2048
FF = 2816
NFC = FF // 128
EPS = 1e-6
TWO_PI = 2.0 * math.pi
GELU_C = 2.0 * math.sqrt(2.0 / math.pi)


class Buf:
    def __init__(self, name):
        self.name = name
        self.lw = None
        self.rd = []
        self.dsem = None
        self.dcnt = 0


class Sched:
    def __init__(self, nc, es):
        self.nc = nc
        self.es = es
        self.eng = {"pe": nc.tensor, "act": nc.scalar, "dve": nc.vector, "pool": nc.gpsimd, "sp": nc.sync}
        self.sem = {k: es.enter_context(nc.semaphore("s_" + k)) for k in self.eng}
        self.cnt = {k: 0 for k in self.eng}
        self.seen = {k: {} for k in self.eng}
        self.final = []
        self.nsem = 0
        self.dma_bufs = []

    def _need(self, e, deps):
        best = {}
        for (h, key, val) in deps:
            if key not in best or best[key][1] < val:
                best[key] = (h, val)
        for key, (h, val) in best.items():
            if self.seen[e].get(key, 0) >= val:
                continue
            self.eng[e].wait_ge(h, val)
            self.seen[e][key] = val

    @staticmethod
    def _deps(reads, writes):
        deps = []
        for b in reads:
            if b.lw is not None:
                deps.append(b.lw)
        for b in writes:
            if b.lw is not None:
                deps.append(b.lw)
            deps.extend(b.rd)
        return deps

    def op(self, e, fn, reads=(), writes=()):
        deps = self._deps(reads, writes)
        if e == "pe":
            deps = [d for d in deps if d[1] != "pe"]
        self._need(e, deps)
        ins = fn(self.eng[e])
        self.cnt[e] += 1
        ins.then_inc(self.sem[e], 1)
        tok = (self.sem[e], e, self.cnt[e])
        for b in writes:
            b.lw = tok
            b.rd = []
        for b in reads:
            b.rd.append(tok)

    def dma(self, q, owner, pairs, reads=(), writes=(), final=False, **kw):
        self._need(q, self._deps(reads, writes))
        if owner.dsem is None:
            owner.dsem = self.es.enter_context(self.nc.semaphore("d%d_%s" % (self.nsem, owner.name)))
            self.nsem += 1
            self.dma_bufs.append(owner)
        for (o, i) in pairs:
            self.eng[q].dma_start(out=o, in_=i, **kw).then_inc(owner.dsem, 16)
            owner.dcnt += 16
        tok = (owner.dsem, "d_" + owner.name, owner.dcnt)
        for b in writes:
            b.lw = tok
            b.rd = []
        for b in reads:
            b.rd.append(tok)
        if final:
            self.final.append(tok)

    def barrier(self):
        toks = [(self.sem[e], e, self.cnt[e]) for e in self.eng if self.cnt[e] > 0]
        toks += [(b.dsem, "d_" + b.name, b.dcnt) for b in self.dma_bufs if b.dcnt > 0]
        for e in self.eng:
            self._need(e, toks)

    def finish(self):
        self._need("sp", self.final)
        toks = [(self.sem[e], e, self.cnt[e]) for e in ("pe", "act", "dve", "pool") if self.cnt[e] > 0]
        self._need("sp", toks)


class Ring:
    def __init__(self, items):
        self.items = items
        self.i = 0

    def next(self):
        it = self.items[self.i % len(self.items)]
        self.i += 1
        return it


def build_program(debug=False, phases="AFSC", stop=None):
    nc = bass.Bass("TRN2", target_bir_lowering=False)
    DBG = "ExternalOutput" if debug else "Internal"
    es = ExitStack()
    S = Sched(nc, es)

    def dram(name, shape, dt, kind):
        return nc.dram_tensor(name, list(shape), dt, kind=kind).ap()

    x_p = dram("x_p", [SP, D], F32, "ExternalInput")
    x_s = dram("x_s", [SS, D], F32, "ExternalInput")
    y_p = dram("y_p", [SP, D], F32, "ExternalOutput")
    y_s = dram("y_s", [OWN, D], F32, "ExternalOutput")
    gains = dram("gains", [4, D], F32, "ExternalInput")
    w_in = dram("w_in", [D, 3072], F32, "ExternalInput")
    w_fn = dram("w_fn", [512, D], F32, "ExternalInput")
    lam_re = dram("lam_re", [2, 32, 64], F32, "ExternalInput")
    lam_im = dram("lam_im", [2, 32, 64], F32, "ExternalInput")
    log_dt = dram("log_dt", [2, 32], F32, "ExternalInput")
    b_re = dram("b_re", [2, 32, 64, 16], F32, "ExternalInput")
    b_im = dram("b_im", [2, 32, 64, 16], F32, "ExternalInput")
    c_re = dram("c_re", [2, 32, 16, 64], F32, "ExternalInput")
    c_im = dram("c_im", [2, 32, 16, 64], F32, "ExternalInput")
    d_skip = dram("d_skip", [512], F32, "ExternalInput")
    w_gv = dram("w_gv", [512, D], F32, "ExternalInput")
    w_gg = dram("w_gg", [512, D], F32, "ExternalInput")
    w_o = dram("w_o", [D, D], F32, "ExternalInput")
    w_fg = dram("w_fg", [D, FF], F32, "ExternalInput")
    w_fu = dram("w_fu", [D, FF], F32, "ExternalInput")
    w_fd = dram("w_fd", [FF, D], F32, "ExternalInput")
    c_identb = dram("c_identb", [128, 128], BF16, "ExternalInput")
    c_identf = dram("c_identf", [128, 128], F32, "ExternalInput")
    c_f1 = dram("c_f1", [128, 2, 128], BF16, "ExternalInput")
    c_gp = dram("c_gp", [64, 128, 192], BF16, "ExternalInput")
    c_gs = dram("c_gs", [128, 128, 48], BF16, "ExternalInput")
    c_cs = dram("c_cs", [128, 256], F32, "ExternalInput")
    c_kv17 = dram("c_kv17", [128, 17], F32, "ExternalInput")
    c_kv129 = dram("c_kv129", [128, 129], F32, "ExternalInput")
    c_mf = dram("c_mf", [128, 128], F32, "ExternalInput")
    c_mb = dram("c_mb", [128, 128], F32, "ExternalInput")
    c_mask = dram("c_mask", [128, 16], F32, "ExternalInput")
    uf_p = dram("uf_p", [SP, 512], BF16, DBG)
    us_p = dram("us_p", [SP, 512], BF16, DBG)
    uf_s = dram("uf_s", [SS, 512], BF16, "Internal")
    us_s = dram("us_s", [SS, 512], BF16, "Internal")
    V_p = dram("V_p", [1024, SP], BF16, DBG)
    V_s = dram("V_s", [1024, OWN], BF16, DBG)
    ys_p = dram("ys_p", [SP, 512], BF16, DBG)
    ys_s = dram("ys_s", [OWN, 512], BF16, DBG)
    U_sc = dram("U_sc", [10, 2, 128, 2048], BF16, "Internal")
    XD_p = dram("XD_p", [8, 2, 128, 4096], BF16, "Internal")
    XD_s = dram("XD_s", [2, 2, 128, 4096], BF16, "Internal")
    wcat = dram("wcat", [60, 128, 8, 128], BF16, "Internal")
    wd_sc = dram("wd_sc", [FF, D], BF16, "Internal")

    B_ufp, B_usp, B_ufs, B_uss = Buf("ufp"), Buf("usp"), Buf("ufs"), Buf("uss")
    B_Vp, B_Vs, B_ysp, B_yss = Buf("Vp"), Buf("Vs"), Buf("ysp"), Buf("yss")
    B_Usc, B_XDp, B_XDs, B_wcat, B_wd = Buf("Usc"), Buf("XDp"), Buf("XDs"), Buf("wcat"), Buf("wdsc")
    B_dummy_out = Buf("yout")

    def sb(stack, name, shape, dt):
        t = stack.enter_context(nc.sbuf_tensor(name, list(shape), dt))
        return t, Buf(name)

    psf = []
    for i in range(6):
        t = es.enter_context(nc.psum_tensor("psf%d" % i, [128, 512], F32))
        psf.append((t, Buf("psf%d" % i)))
    psb = []
    for i in range(2):
        t = es.enter_context(nc.psum_tensor("psb%d" % i, [128, 1024], BF16))
        psb.append((t, Buf("psb%d" % i)))
    PF = Ring(psf)
    PB = Ring(psb)

    identb, B_identb = sb(es, "identb", [128, 128], BF16)
    identf, B_identf = sb(es, "identf", [128, 128], F32)
    gt, B_gt = sb(es, "gt", [128, 4, 8], F32)
    epst, B_epst = sb(es, "epst", [128, 1], F32)
    S.op("dve", lambda e: e.memset(epst[:], EPS), writes=[B_epst])

    def rsq(ap, B_ap):
        S.op("act", lambda e: e.activation(out=ap, in_=ap, func=AF.Sqrt, scale=1.0 / D, bias=epst[:, 0:1]),
             reads=[B_ap, B_epst], writes=[B_ap])
        S.op("dve", lambda e: e.reciprocal(out=ap, in_=ap), reads=[B_ap], writes=[B_ap])
    S.dma("sp", B_identb, [(identb[:], c_identb[:, :])], writes=[B_identb])
    S.dma("sp", B_identf, [(identf[:], c_identf[:, :])], writes=[B_identf])
    S.dma("sp", B_gt, [(gt[:, w, :], gains[w].rearrange("(c p) -> p c", p=128)) for w in range(4)],
          writes=[B_gt], allow_slow_non_contiguous=True)

    ph_a = ExitStack()
    wu_res, B_wu = sb(ph_a, "wu_res", [128, 8, 1024], BF16)
    with ExitStack() as st:
        ld = [sb(st, "cv_ld%d" % i, [128, 3072], F32) for i in range(2)]
        cv = [sb(st, "cv_o%d" % i, [128, 2816], BF16) for i in range(2)]
        LD, CV = Ring(ld), Ring(cv)
        for dc in range(8):
            lt, lb = LD.next()
            S.dma("sp", lb, [(lt[:, 0:3072], w_in[dc * 128:(dc + 1) * 128, :])], writes=[lb])
            S.op("act", lambda e, lt=lt, dc=dc: e.activation(out=wu_res[:, dc, :], in_=lt[:, 0:1024], func=AF.Copy,
                                                             scale=gt[:, 0, dc:dc + 1]),
                 reads=[lb, B_gt], writes=[B_wu])
            ct, cb = CV.next()
            S.op("act", lambda e, lt=lt, ct=ct, dc=dc: e.activation(out=ct[:, 0:2048], in_=lt[:, 1024:3072], func=AF.Copy,
                                                                    scale=gt[:, 0, dc:dc + 1]),
                 reads=[lb, B_gt], writes=[cb])
            S.dma("pool", cb, [(wcat[0:16, :, dc, :].rearrange("m p f -> p m f"),
                                ct[:, 0:2048].rearrange("p (m f) -> p m f", f=128))],
                  reads=[cb], writes=[B_wcat])
        for wi, (wsrc, m0) in enumerate(((w_fg, 16), (w_fu, 38))):
            for dc in range(8):
                lt, lb = LD.next()
                S.dma("sp", lb, [(lt[:, 0:FF], wsrc[dc * 128:(dc + 1) * 128, :])], writes=[lb])
                ct, cb = CV.next()
                S.op("act", lambda e, lt=lt, ct=ct, dc=dc: e.activation(out=ct[:, 0:FF], in_=lt[:, 0:FF], func=AF.Copy,
                                                                        scale=gt[:, 2, dc:dc + 1]),
                     reads=[lb, B_gt], writes=[cb])
                S.dma("pool", cb, [(wcat[m0:m0 + NFC, :, dc, :].rearrange("m p f -> p m f"),
                                    ct[:, 0:FF].rearrange("p (m f) -> p m f", f=128))],
                      reads=[cb], writes=[B_wcat])
        for fc in range(NFC):
            lt, lb = LD.next()
            S.dma("sp", lb, [(lt[:, 0:1024], w_fd[fc * 128:(fc + 1) * 128, :])], writes=[lb])
            ct, cb = CV.next()
            S.op("act", lambda e, lt=lt, ct=ct: e.activation(out=ct[:, 0:1024], in_=lt[:, 0:1024], func=AF.Copy),
                 reads=[lb], writes=[cb])
            S.dma("pool", cb, [(wd_sc[fc * 128:(fc + 1) * 128, :], ct[:, 0:1024])], reads=[cb], writes=[B_wd])

    S.barrier()
    if stop == "0":
        S.finish()
        return nc
    def rms_rstd(stack_tiles, src_ap_fn, nk, ss, B_ss, rstd, B_rstd, junk, B_junk, reads):
        for k in range(nk):
            S.op("act", lambda e, k=k: e.activation(out=junk[:], in_=src_ap_fn(k), func=AF.Square,
                                                    accum_out=ss[:, k:k + 1]),
                 reads=reads, writes=[B_junk, B_ss])
        S.op("dve", lambda e: e.tensor_copy(out=rstd[:, 0:nk], in_=ss[:, 0:nk]), reads=[B_ss], writes=[B_rstd])
        rsq(rstd[:, 0:nk], B_rstd)

    def norm_hb(xt, B_xt, rstd, B_rstd, hb, B_hb):
        for k in range(4):
            S.op("act", lambda e, k=k: e.activation(out=hb[:, k, :], in_=xt[:, k, :], func=AF.Copy,
                                                    scale=rstd[:, k:k + 1]),
                 reads=[B_xt, B_rstd], writes=[B_hb])

    def transpose_hT(hb, B_hb, hT, B_hT):
        for dp in range(4):
            pt, pb = PB.next()

            def tr(e, dp=dp, pt=pt):
                ins = None
                for d2 in range(2):
                    for k in range(4):
                        dc = dp * 2 + d2
                        ins = e.transpose(pt[:, (d2 * 4 + k) * 128:(d2 * 4 + k + 1) * 128],
                                          hb[:, k, dc * 128:(dc + 1) * 128], identb[:])
                return ins
            S.op("pe", tr, reads=[B_hb, B_identb], writes=[pb])
            S.op("dve", lambda e, dp=dp, pt=pt: e.tensor_copy(
                out=hT[:, dp * 2:dp * 2 + 2, :], in_=pt[:].rearrange("p (a t) -> p a t", a=2)),
                reads=[pb], writes=[B_hT])

    def norm_transpose(xt, B_xt, rstd, B_rstd, hb, B_hb, hT, B_hT):
        norm_hb(xt, B_xt, rstd, B_rstd, hb, B_hb)
        transpose_hT(hb, B_hb, hT, B_hT)

    with ExitStack() as st:
      if "A" in phases:
          xts = [sb(st, "a_xt%d" % i, [128, 4, 1024], F32) for i in range(2)]
          XT = Ring(xts)
          HB = Ring([sb(st, "a_hb%d" % i, [128, 4, 1024], BF16) for i in range(2)])
          HT = Ring([sb(st, "a_hT%d" % i, [128, 8, 512], BF16) for i in range(2)])
          uos = [sb(st, "a_uo%d" % i, [128, 4, 1024], BF16) for i in range(2)]
          UO = Ring(uos)
          SSR = Ring([sb(st, "a_ss%d" % i, [128, 4], F32) for i in range(2)])
          RSR = Ring([sb(st, "a_rstd%d" % i, [128, 4], F32) for i in range(2)])
          JK = Ring([sb(st, "a_junk%d" % i, [128, 1024], F32) for i in range(2)])
          jobs = [(x_p, uf_p, us_p, B_ufp, B_usp, t) for t in range(SP // 512)] + \
                 [(x_s, uf_s, us_s, B_ufs, B_uss, t) for t in range(SS // 512)]
          if "a" in phases:
              jobs = jobs[:3]

          def a_front_act(job):
              (xsrc, ufd, usd, Bf, Bs, t) = job
              xt, B_xt = XT.next()
              hb, B_hb = HB.next()
              ss, B_ss = SSR.next()
              rstd, B_rstd = RSR.next()
              junk, B_junk = JK.next()
              S.dma("sp", B_xt, [(xt[:], xsrc[t * 512:(t + 1) * 512, :].rearrange("(k p) d -> p k d", p=128))],
                    writes=[B_xt])
              rms_rstd(None, lambda k, xt=xt: xt[:, k, :], 4, ss, B_ss, rstd, B_rstd, junk, B_junk, [B_xt])
              norm_hb(xt, B_xt, rstd, B_rstd, hb, B_hb)
              return (hb, B_hb)

          def a_front_pe(fr):
              hb, B_hb = fr
              hT, B_hT = HT.next()
              transpose_hT(hb, B_hb, hT, B_hT)
              return (hT, B_hT)

          def a_back(job, hTt):
              (xsrc, ufd, usd, Bf, Bs, t) = job
              hT, B_hT = hTt
              uo, B_uo = UO.next()
              for k in range(4):
                  for hf in range(2):
                      pt, pb = PF.next()

                      def mm(e, k=k, hf=hf, pt=pt):
                          ins = None
                          for dc in range(8):
                              ins = e.matmul(pt[:], lhsT=hT[:, dc, k * 128:(k + 1) * 128],
                                             rhs=wu_res[:, dc, hf * 512:(hf + 1) * 512],
                                             start=(dc == 0), stop=(dc == 7))
                          return ins
                      S.op("pe", mm, reads=[B_hT, B_wu], writes=[pb])
                      S.op("act", lambda e, k=k, hf=hf, pt=pt, uo=uo: e.activation(
                          out=uo[:, k, hf * 512:(hf + 1) * 512], in_=pt[:], func=AF.Copy),
                          reads=[pb], writes=[B_uo])
              rows = slice(t * 512, (t + 1) * 512)
              S.dma("pool", B_uo, [(ufd[rows, :].rearrange("(k p) c -> p k c", p=128), uo[:, :, 0:512]),
                                   (usd[rows, :].rearrange("(k p) c -> p k c", p=128), uo[:, :, 512:1024])],
                    reads=[B_uo], writes=[Bf, Bs])

          fr = a_front_act(jobs[0])
          hTt = a_front_pe(fr)
          for ji, job in enumerate(jobs):
              fr = a_front_act(jobs[ji + 1]) if ji + 1 < len(jobs) else None
              a_back(job, hTt)
              if fr is not None:
                  hTt = a_front_pe(fr)
    S.barrier()
    ph_a.close()
    if stop == "A":
        S.finish()
        return nc

    def fft_seq(st, tag, ufd, Bf, Vd, BV, N2, K2, gtab_dram):
        SO = K2 * 128
        W2 = 2 * K2
        nslot = 512 // W2
        f1, B_f1 = sb(st, tag + "f1", [128, 2, 128], BF16)
        gtab, B_gtab = sb(st, tag + "gt", [N2, 128, 3 * K2], BF16)
        S.dma("sp", B_f1, [(f1[:], c_f1[:, :, :])], writes=[B_f1])
        S.dma("sp", B_gtab, [(gtab[:], gtab_dram[:, :, :])], writes=[B_gtab])
        xgs = [sb(st, tag + "xg%d" % i, [128, N2, 128], BF16) for i in range(2)]
        XG = Ring(xgs)
        aps = [sb(st, tag + "ap%d" % i, [N2, 128, 2, 64], BF16) for i in range(2)]
        APR = Ring(aps)
        vts = [sb(st, tag + "vt%d" % i, [128, 2, SO], BF16) for i in range(2 if SO <= 2048 else 1)]
        VT = Ring(vts)
        for g in range(4):
            xg, B_xg = XG.next()
            S.dma("sp", B_xg, [(xg[:], ufd[:, g * 128:(g + 1) * 128].rearrange("(a b) c -> a b c", b=N2))],
                  reads=[Bf], writes=[B_xg])
            vt, B_vt = VT.next()
            for kh in range(2):
                apt, B_ap = APR.next()
                for cq in range(32):
                    pt, pb = PF.next()

                    def s1(e, cq=cq, pt=pt, xg=xg, kh=kh):
                        ins = None
                        for c4 in range(4):
                            ins = e.matmul(pt[0:N2, c4 * 128:(c4 + 1) * 128], lhsT=xg[:, :, cq * 4 + c4],
                                           rhs=f1[:, kh, :], start=True, stop=True)
                        return ins
                    S.op("pe", s1, reads=[B_xg, B_f1], writes=[pb])
                    eng = "act" if (cq % 2 == 0) else "dve"
                    src = pt[0:N2, :]
                    dst = apt[:, cq * 4:cq * 4 + 4, :, :].rearrange("p c r k -> p (c r k)")
                    if eng == "act":
                        S.op("act", lambda e, src=src, dst=dst: e.activation(out=dst, in_=src, func=AF.Copy),
                             reads=[pb], writes=[B_ap])
                    else:
                        S.op("dve", lambda e, src=src, dst=dst: e.tensor_copy(out=dst, in_=src),
                             reads=[pb], writes=[B_ap])
                for kb in range(64 // nslot):
                    pt, pb = PF.next()

                    def s2(e, kb=kb, pt=pt, apt=apt, kh=kh):
                        ins = None
                        for sl in range(nslot):
                            k1h = kb * nslot + sl
                            k1 = kh * 64 + k1h
                            e.matmul(pt[:, sl * W2:(sl + 1) * W2], lhsT=apt[:, :, 0, k1h], rhs=gtab[:, k1, 0:W2],
                                     start=True, stop=False)
                            ins = e.matmul(pt[:, sl * W2:(sl + 1) * W2], lhsT=apt[:, :, 1, k1h],
                                           rhs=gtab[:, k1, K2:3 * K2], start=False, stop=True)
                        return ins
                    S.op("pe", s2, reads=[B_ap, B_gtab], writes=[pb])
                    k1_0 = kh * 64 + kb * nslot
                    src = pt[:].rearrange("p (s r k) -> p s r k", s=nslot, r=2)
                    dst = vt[:].rearrange("p r (k q) -> p q r k", q=128)[:, k1_0:k1_0 + nslot, :, :]
                    if kb % 2 == 0:
                        S.op("act", lambda e, src=src, dst=dst: e.activation(out=dst, in_=src, func=AF.Copy),
                             reads=[pb], writes=[B_vt])
                    else:
                        S.op("dve", lambda e, src=src, dst=dst: e.tensor_copy(out=dst, in_=src),
                             reads=[pb], writes=[B_vt])
            S.dma("pool", B_vt, [(Vd[(g * 2 + r) * 128:(g * 2 + r + 1) * 128, :], vt[:, r, :]) for r in range(2)],
                  reads=[B_vt], writes=[BV])

    if "F" in phases:
        with ExitStack() as st:
            fft_seq(st, "fp_", uf_p, B_ufp, V_p, B_Vp, 64, 64, c_gp)
            S.barrier()
        with ExitStack() as st:
            fft_seq(st, "fs_", uf_s, B_ufs, V_s, B_Vs, 128, 16, c_gs)
            S.barrier()

    ph_b = ExitStack()
    wzt = {}
    for nm in ("rf", "rb", "if", "ib"):
        wzt[nm] = sb(ph_b, "wz_" + nm, [128, 32, 128], BF16)
    woRf, B_woRf = sb(ph_b, "woRf", [128, 32, 128], BF16)
    woRb, B_woRb = sb(ph_b, "woRb", [128, 32, 128], BF16)
    woIf, B_woIf = sb(ph_b, "woIf", [128, 32, 128], BF16)
    woIb, B_woIb = sb(ph_b, "woIb", [128, 32, 128], BF16)
    kloc, B_kloc = sb(ph_b, "kloc", [128, 32, 128], BF16)
    av, B_av = sb(ph_b, "av_p", [128, 32], F32)
    th, B_th = sb(ph_b, "th_p", [128, 32], F32)
    maskt, B_maskt = sb(ph_b, "maskt", [128, 16], F32)
    S.dma("sp", B_maskt, [(maskt[:], c_mask[:, :])], writes=[B_maskt])

    def reduce_pi(st, tag, dst, B_dst, src, B_src, shape, shift, tmps=None):
        if tmps is None:
            tmps = (sb(st, tag + "_ri", shape, mybir.dt.int32), sb(st, tag + "_rf", shape, F32))
        (ti, B_ti), (tf, B_tf) = tmps
        S.op("dve", lambda e: e.tensor_scalar(out=dst, in0=src, scalar1=shift, scalar2=None, op0=ALU.add),
             reads=[B_src], writes=[B_dst])
        S.op("dve", lambda e: e.tensor_scalar(out=tf[:], in0=dst, scalar1=1.0 / TWO_PI, scalar2=None, op0=ALU.mult),
             reads=[B_dst], writes=[B_tf])
        S.op("dve", lambda e: e.tensor_copy(out=ti[:], in_=tf[:]), reads=[B_tf], writes=[B_ti])
        S.op("dve", lambda e: e.tensor_copy(out=tf[:], in_=ti[:]), reads=[B_ti], writes=[B_tf])
        S.op("dve", lambda e: e.scalar_tensor_tensor(out=dst, in0=tf[:], scalar=-TWO_PI, in1=dst, op0=ALU.mult, op1=ALU.add),
             reads=[B_tf, B_dst], writes=[B_dst])
        for (cmp_, thr, add) in ((ALU.is_gt, math.pi, -TWO_PI), (ALU.is_lt, -math.pi, TWO_PI)):
            S.op("dve", lambda e, cmp_=cmp_, thr=thr: e.tensor_scalar(out=tf[:], in0=dst, scalar1=thr, scalar2=None, op0=cmp_),
                 reads=[B_dst], writes=[B_tf])
            S.op("dve", lambda e, add=add: e.scalar_tensor_tensor(out=dst, in0=tf[:], scalar=add, in1=dst, op0=ALU.mult, op1=ALU.add),
                 reads=[B_tf, B_dst], writes=[B_dst])
        S.op("dve", lambda e: e.tensor_scalar(out=dst, in0=dst, scalar1=3.14159, scalar2=-3.14159, op0=ALU.min, op1=ALU.max),
             reads=[B_dst], writes=[B_dst])

    def sincos(st, tag, ang, B_ang, shape, cos_out, B_cos, sin_out, B_sin, off):
        tmp, B_tmp = sb(st, tag + "_sc", shape, F32)
        tmps = (sb(st, tag + "_ri", shape, mybir.dt.int32), sb(st, tag + "_rf", shape, F32))
        reduce_pi(st, tag + "s", tmp[:], B_tmp, ang, B_ang, shape, 0.0, tmps)
        S.op("act", lambda e: e.activation(out=sin_out, in_=tmp[:], func=AF.Sin), reads=[B_tmp], writes=[B_sin])
        reduce_pi(st, tag + "c", tmp[:], B_tmp, ang, B_ang, shape, 0.5 * math.pi, tmps)
        S.op("act", lambda e: e.activation(out=cos_out, in_=tmp[:], func=AF.Sin), reads=[B_tmp], writes=[B_cos])

    negpi, B_negpi = sb(ph_b, "negpi", [128, 1], F32)
    S.op("dve", lambda e: e.memset(negpi[:], -math.pi), writes=[B_negpi])

    with ExitStack() as st:
        def T(name, shape, dt=F32):
            return sb(st, "tg_" + name, shape, dt)

        def dv(fn, reads, writes):
            S.op("dve", fn, reads=reads, writes=writes)

        def tt(out, a, b, op, reads, writes):
            dv(lambda e: e.tensor_tensor(out=out, in0=a, in1=b, op=op), reads, writes)

        lr, B_lr = T("lr", [128, 32])
        li, B_li = T("li", [128, 32])
        ldt, B_ldt = T("ldt", [128, 32])
        btr, B_btr = T("btr", [128, 32, 16])
        bti, B_bti = T("bti", [128, 32, 16])
        craw_r, B_crr = T("crr", [128, 4, 128])
        craw_i, B_cri = T("cri", [128, 4, 128])
        ctr, B_ctr = T("ctr", [128, 32, 16])
        cti, B_cti = T("cti", [128, 32, 16])
        dsk, B_dsk = T("dsk", [128, 32])
        kv17, B_kv17 = T("kv17", [128, 17])
        mf, B_mf = T("mf", [128, 128])
        mb, B_mb = T("mb", [128, 128])
        S.dma("sp", B_lr, [(lr[d * 64:(d + 1) * 64, :], lam_re[d].rearrange("g p -> p g")) for d in range(2)],
              writes=[B_lr], allow_slow_non_contiguous=True)
        S.dma("sp", B_li, [(li[d * 64:(d + 1) * 64, :], lam_im[d].rearrange("g p -> p g")) for d in range(2)],
              writes=[B_li], allow_slow_non_contiguous=True)
        S.dma("sp", B_ldt, [(ldt[d * 64:(d + 1) * 64, :], log_dt[d:d + 1, :].broadcast_to([64, 32])) for d in range(2)],
              writes=[B_ldt])
        S.dma("sp", B_btr, [(btr[d * 64:(d + 1) * 64, :, :], b_re[d].rearrange("g p h -> p g h")) for d in range(2)],
              writes=[B_btr])
        S.dma("sp", B_bti, [(bti[d * 64:(d + 1) * 64, :, :], b_im[d].rearrange("g p h -> p g h")) for d in range(2)],
              writes=[B_bti])
        S.dma("sp", B_crr, [(craw_r[:, q, d * 64:(d + 1) * 64],
                             c_re[d, q * 8:(q + 1) * 8].rearrange("g h p -> (g h) p"))
                            for q in range(4) for d in range(2)], writes=[B_crr])
        S.dma("sp", B_cri, [(craw_i[:, q, d * 64:(d + 1) * 64],
                             c_im[d, q * 8:(q + 1) * 8].rearrange("g h p -> (g h) p"))
                            for q in range(4) for d in range(2)], writes=[B_cri])
        S.dma("sp", B_dsk, [(dsk[i * 16:(i + 1) * 16, :], d_skip.rearrange("(g h) -> h g", h=16)) for i in range(8)],
              writes=[B_dsk], allow_slow_non_contiguous=True)
        S.dma("sp", B_kv17, [(kv17[:], c_kv17[:, :])], writes=[B_kv17])
        S.dma("sp", B_mf, [(mf[:], c_mf[:, :])], writes=[B_mf])
        S.dma("sp", B_mb, [(mb[:], c_mb[:, :])], writes=[B_mb])
        crawb, B_crawb = T("crawb", [128, 4, 128], BF16)
        for (craw, B_craw, ct_, B_ct) in ((craw_r, B_crr, ctr, B_ctr), (craw_i, B_cri, cti, B_cti)):
            dv(lambda e, craw=craw: e.tensor_copy(out=crawb[:], in_=craw[:]), [B_craw], [B_crawb])
            pt, pb = PB.next()

            def trc(e, pt=pt):
                ins = None
                for q in range(4):
                    ins = e.transpose(pt[:, q * 128:(q + 1) * 128], crawb[:, q, :], identb[:])
                return ins
            S.op("pe", trc, reads=[B_crawb, B_identb], writes=[pb])
            S.op("dve", lambda e, pt=pt, ct_=ct_: e.tensor_copy(out=ct_[:].rearrange("p g h -> p (g h)"), in_=pt[:, 0:512]),
                 reads=[pb], writes=[B_ct])
        dtt, B_dtt = T("dtt", [128, 32])
        S.op("act", lambda e: e.activation(out=dtt[:], in_=ldt[:], func=AF.Exp), reads=[B_ldt], writes=[B_dtt])
        tt(av[:], lr[:], dtt[:], ALU.mult, [B_lr, B_dtt], [B_av])
        tt(th[:], li[:], dtt[:], ALU.mult, [B_li, B_dtt], [B_th])
        if stop == "T1":
            S.finish()
            return nc
        ak, B_ak = T("ak", [128, 32, 17])
        tk, B_tk = T("tk", [128, 32, 17])
        mag, B_mag = T("mag", [128, 32, 17])
        pwr, B_pwr = T("pwr", [128, 32, 17])
        pwi, B_pwi = T("pwi", [128, 32, 17])
        kvb = kv17[:].unsqueeze(1).broadcast_to([128, 32, 17])
        tt(ak[:], av[:].unsqueeze(2).broadcast_to([128, 32, 17]), kvb, ALU.mult, [B_av, B_kv17], [B_ak])
        tt(tk[:], th[:].unsqueeze(2).broadcast_to([128, 32, 17]), kvb, ALU.mult, [B_th, B_kv17], [B_tk])
        S.op("act", lambda e: e.activation(out=mag[:], in_=ak[:], func=AF.Exp), reads=[B_ak], writes=[B_mag])
        sincos(st, "pw", tk[:], B_tk, [128, 32, 17], pwr[:], B_pwr, pwi[:], B_pwi, 64 * math.pi)
        tt(pwr[:], pwr[:], mag[:], ALU.mult, [B_pwr, B_mag], [B_pwr])
        tt(pwi[:], pwi[:], mag[:], ALU.mult, [B_pwi, B_mag], [B_pwi])
        if stop == "T2":
            S.finish()
            return nc
        n2, B_n2 = T("n2", [128, 32])
        t1, B_t1 = T("t1", [128, 32])
        t2, B_t2 = T("t2", [128, 32])
        qr, B_qr = T("qr", [128, 32])
        qi, B_qi = T("qi", [128, 32])
        l1r, B_l1r = T("l1r", [128, 32])
        tt(n2[:], lr[:], lr[:], ALU.mult, [B_lr], [B_n2])
        tt(t1[:], li[:], li[:], ALU.mult, [B_li], [B_t1])
        tt(n2[:], n2[:], t1[:], ALU.add, [B_n2, B_t1], [B_n2])
        dv(lambda e: e.reciprocal(out=n2[:], in_=n2[:]), [B_n2], [B_n2])
        dv(lambda e: e.tensor_scalar(out=l1r[:], in0=pwr[:, :, 9], scalar1=-1.0, scalar2=None, op0=ALU.add),
           [B_pwr], [B_l1r])
        tt(t1[:], l1r[:], lr[:], ALU.mult, [B_l1r, B_lr], [B_t1])
        tt(t2[:], pwi[:, :, 9], li[:], ALU.mult, [B_pwi, B_li], [B_t2])
        tt(qr[:], t1[:], t2[:], ALU.add, [B_t1, B_t2], [B_qr])
        tt(qr[:], qr[:], n2[:], ALU.mult, [B_qr, B_n2], [B_qr])
        tt(t1[:], pwi[:, :, 9], lr[:], ALU.mult, [B_pwi, B_lr], [B_t1])
        tt(t2[:], l1r[:], li[:], ALU.mult, [B_l1r, B_li], [B_t2])
        tt(qi[:], t1[:], t2[:], ALU.subtract, [B_t1, B_t2], [B_qi])
        tt(qi[:], qi[:], n2[:], ALU.mult, [B_qi, B_n2], [B_qi])
        bbr, B_bbr = T("bbr", [128, 32, 16])
        bbi, B_bbi = T("bbi", [128, 32, 16])
        t3, B_t3 = T("t3", [128, 32, 16])
        qrb = qr[:].unsqueeze(2).broadcast_to([128, 32, 16])
        qib = qi[:].unsqueeze(2).broadcast_to([128, 32, 16])
        tt(bbr[:], btr[:], qrb, ALU.mult, [B_btr, B_qr], [B_bbr])
        tt(t3[:], bti[:], qib, ALU.mult, [B_bti, B_qi], [B_t3])
        tt(bbr[:], bbr[:], t3[:], ALU.subtract, [B_bbr, B_t3], [B_bbr])
        tt(bbi[:], bti[:], qrb, ALU.mult, [B_bti, B_qr], [B_bbi])
        tt(t3[:], btr[:], qib, ALU.mult, [B_btr, B_qi], [B_t3])
        tt(bbi[:], bbi[:], t3[:], ALU.add, [B_bbi, B_t3], [B_bbi])

        if stop == "T3":
            S.finish()
            return nc
        big1, B_big1 = T("big1", [128, 32, 8, 16])

        def pw_slices(kind):
            if kind == "wo":
                return (slice(9, 17), slice(16, 8, -1))
            if kind == "bn":
                return (slice(7, None, -1), slice(0, 8))
            if kind == "bz":
                return (slice(15, 7, -1), slice(8, 16))
            raise ValueError(kind)

        def cprod(outR, B_outR, outI, B_outI, kind, mr, B_mr, mi, B_mi, neg_im):
            sl = pw_slices(kind)
            for half in range(2):
                ps_ = slice(half * 64, (half + 1) * 64)
                pr = pwr[ps_, :, sl[half]].unsqueeze(3).broadcast_to([64, 32, 8, 16])
                pi_ = pwi[ps_, :, sl[half]].unsqueeze(3).broadcast_to([64, 32, 8, 16])
                mrb = mr[ps_].unsqueeze(2).broadcast_to([64, 32, 8, 16])
                mib = mi[ps_].unsqueeze(2).broadcast_to([64, 32, 8, 16])
                oR = outR[ps_].rearrange("p g (x y) -> p g x y", y=16)
                oI = outI[ps_].rearrange("p g (x y) -> p g x y", y=16)
                b1 = big1[ps_]
                tt(oR, pr, mrb, ALU.mult, [B_pwr, B_mr], [B_outR])
                tt(b1, pi_, mib, ALU.mult, [B_pwi, B_mi], [B_big1])
                tt(oR, oR, b1, ALU.subtract, [B_outR, B_big1], [B_outR])
                tt(oI, pr, mib, ALU.mult, [B_pwr, B_mi], [B_outI])
                tt(b1, pi_, mrb, ALU.mult, [B_pwi, B_mr], [B_big1])
                if neg_im:
                    tt(oI, oI, b1, ALU.add, [B_outI, B_big1], [B_outI])
                    dv(lambda e, oI=oI: e.tensor_scalar(out=oI, in0=oI, scalar1=-1.0, scalar2=None, op0=ALU.mult),
                       [B_outI], [B_outI])
                else:
                    tt(oI, oI, b1, ALU.add, [B_outI, B_big1], [B_outI])

        woR32, B_woR32 = T("woR32", [128, 32, 128])
        woI32, B_woI32 = T("woI32", [128, 32, 128])
        bnR, B_bnR = T("bnR", [128, 32, 128])
        bnI, B_bnI = T("bnI", [128, 32, 128])
        cprod(woR32, B_woR32, woI32, B_woI32, "wo", ctr, B_ctr, cti, B_cti, True)
        for (tl, B_tl) in ((woRf, B_woRf), (woRb, B_woRb), (woIf, B_woIf), (woIb, B_woIb)):
            S.op("pool", lambda e, tl=tl: e.memset(tl[:], 0.0), writes=[B_tl])
        dv(lambda e: e.tensor_copy(out=woRf[0:64], in_=woR32[0:64]), [B_woR32, B_woRf], [B_woRf])
        dv(lambda e: e.tensor_copy(out=woRb[64:128], in_=woR32[64:128]), [B_woR32, B_woRb], [B_woRb])
        dv(lambda e: e.tensor_copy(out=woIf[0:64], in_=woI32[0:64]), [B_woI32, B_woIf], [B_woIf])
        dv(lambda e: e.tensor_copy(out=woIb[64:128], in_=woI32[64:128]), [B_woI32, B_woIb], [B_woIb])
        cprod(bnR, B_bnR, bnI, B_bnI, "bn", bbr, B_bbr, bbi, B_bbi, False)
        if stop == "T4":
            S.finish()
            return nc
        class _V:
            def __init__(self, ap):
                self.ap = ap

            def __getitem__(self, k):
                return self.ap[k]
        bnRb = _V(woR32[:].rearrange("p g c -> p (g c)").bitcast(BF16)[:, 0:4096].rearrange("p (g c) -> p g c", g=32))
        bnIb = _V(woI32[:].rearrange("p g c -> p (g c)").bitcast(BF16)[:, 0:4096].rearrange("p (g c) -> p g c", g=32))
        B_bnRb, B_bnIb = B_woR32, B_woI32
        dv(lambda e: e.tensor_copy(out=bnRb[:], in_=bnR[:]), [B_bnR], [B_bnRb])
        dv(lambda e: e.tensor_copy(out=bnIb[:], in_=bnI[:]), [B_bnI], [B_bnIb])
        ktmp, B_ktmp = T("ktmp", [128, 128])
        ktmp2, B_ktmp2 = T("ktmp2", [128, 128])
        for g in range(32):
            ptf, pbf = PF.next()

            def kmm(e, g=g, pt=ptf):
                for half, (wr_, wi_) in enumerate(((woRf, woIf), (woRb, woIb))):
                    e.matmul(pt[:, half * 128:(half + 1) * 128], lhsT=bnRb[:, g, :], rhs=wr_[:, g, :],
                             start=True, stop=False)
                    ins = e.matmul(pt[:, half * 128:(half + 1) * 128], lhsT=bnIb[:, g, :], rhs=wi_[:, g, :],
                                   start=False, stop=True)
                return ins
            S.op("pe", kmm, reads=[B_bnRb, B_bnIb, B_woRf, B_woRb, B_woIf, B_woIb], writes=[pbf])
            tt(ktmp[:], ptf[:, 0:128], mf[:], ALU.mult, [pbf, B_mf], [B_ktmp])
            tt(ktmp2[:], ptf[:, 128:256], mb[:], ALU.mult, [pbf, B_mb], [B_ktmp2])
            tt(ktmp[:], ktmp[:], ktmp2[:], ALU.add, [B_ktmp, B_ktmp2], [B_ktmp])
            dv(lambda e, g=g: e.scalar_tensor_tensor(out=kloc[:, g, :], in0=identf[:], scalar=dsk[:, g:g + 1],
                                                     in1=ktmp[:], op0=ALU.mult, op1=ALU.add),
               [B_identf, B_dsk, B_ktmp], [B_kloc])
        if stop == "T5":
            S.finish()
            return nc
        cprod(bnR, B_bnR, bnI, B_bnI, "bz", bbr, B_bbr, bbi, B_bbi, False)
        for nm in ("rf", "rb", "if", "ib"):
            S.op("pool", lambda e, nm=nm: e.memset(wzt[nm][0][:], 0.0), writes=[wzt[nm][1]])
        dv(lambda e: e.tensor_copy(out=bnRb[:], in_=bnR[:]), [B_bnR], [B_bnRb])
        dv(lambda e: e.tensor_copy(out=bnIb[:], in_=bnI[:]), [B_bnI], [B_bnIb])
        for (src, B_src, kf, kb_) in ((bnRb, B_bnRb, "rf", "rb"), (bnIb, B_bnIb, "if", "ib")):
            for gq in range(4):
                pt, pb = PB.next()

                def trz(e, src=src, gq=gq, pt=pt):
                    ins = None
                    for g8 in range(8):
                        ins = e.transpose(pt[:, g8 * 128:(g8 + 1) * 128], src[:, gq * 8 + g8, :], identb[:])
                    return ins
                S.op("pe", trz, reads=[B_src, B_identb], writes=[pb])
                v = pt[:].rearrange("p (g c) -> p g c", g=8)
                dv(lambda e, v=v, kf=kf, gq=gq: e.tensor_copy(out=wzt[kf][0][:, gq * 8:gq * 8 + 8, 0:64], in_=v[:, :, 0:64]),
                   [pb], [wzt[kf][1]])
                dv(lambda e, v=v, kb_=kb_, gq=gq: e.tensor_copy(out=wzt[kb_][0][:, gq * 8:gq * 8 + 8, 64:128],
                                                               in_=v[:, :, 64:128]),
                   [pb], [wzt[kb_][1]])
    S.barrier()
    cosT, B_cosT = sb(ph_b, "cosT", [128, 32, 129], F32)
    sinT, B_sinT = sb(ph_b, "sinT", [128, 32, 129], F32)
    dec, B_dec = sb(ph_b, "dec", [128, 32, 128], F32)
    rho, B_rho = sb(ph_b, "rho", [128, 32], F32)
    with ExitStack() as st:
        def dv(fn, reads, writes):
            S.op("dve", fn, reads=reads, writes=writes)
        kv129, B_kv129 = sb(st, "tg_kv129", [128, 129], F32)
        S.dma("sp", B_kv129, [(kv129[:], c_kv129[:, :])], writes=[B_kv129])
        phr, B_phr = sb(st, "tg_phr", [128, 32], F32)
        angk, B_angk = sb(st, "tg_angk", [128, 32, 129], F32)
        ph8, B_ph8 = sb(st, "tg_ph8", [128, 32], F32)
        dv(lambda e: e.tensor_scalar(out=ph8[:], in0=th[:], scalar1=8.0, scalar2=None, op0=ALU.mult), [B_th], [B_ph8])
        reduce_pi(st, "ph8", phr[:], B_phr, ph8[:], B_ph8, [128, 32], 0.0)
        dv(lambda e: e.tensor_tensor(out=angk[:], in0=phr[:].unsqueeze(2).broadcast_to([128, 32, 129]),
                                     in1=kv129[:].unsqueeze(1).broadcast_to([128, 32, 129]), op=ALU.mult),
           [B_phr, B_kv129], [B_angk])
        sincos(st, "ph", angk[:], B_angk, [128, 32, 129], cosT[:], B_cosT, sinT[:], B_sinT, 0.0)
        S.op("act", lambda e: e.activation(out=rho[:], in_=av[:], func=AF.Exp, scale=8.0), reads=[B_av], writes=[B_rho])
        dv(lambda e: e.tensor_copy(out=dec[:], in_=rho[:].unsqueeze(2).broadcast_to([128, 32, 128])), [B_rho], [B_dec])
        dv(lambda e: e.memset(dec[:, :, 0:1], 0.0), [B_dec], [B_dec])

    S.barrier()
    if stop == "T":
        S.finish()
        return nc
    def ssm_pass1(tag, usd, Bus, nsb, f_order, b_order, own, XDd, BXD, Ubase, use_mask):
        with ExitStack() as st:
            tfs = [sb(st, tag + "T%d" % i, [128, 8, 512], BF16) for i in range(2)]
            tgs = [sb(st, tag + "G%d" % i, [128, 32, 128], BF16) for i in range(2)]
            uf_, B_uf_ = sb(st, tag + "Uf", [128, 16, 128], BF16)
            ub_, B_ub_ = sb(st, tag + "Ub", [128, 16, 128], BF16)
            ztr, B_ztr = sb(st, tag + "ztr", [128, 16, 128], F32)
            zti, B_zti = sb(st, tag + "zti", [128, 16, 128], F32)
            sr, B_sr = sb(st, tag + "sr", [128, 16, 128], F32)
            si, B_si = sb(st, tag + "si", [128, 16, 128], F32)
            dd, B_dd = sb(st, tag + "dd", [128, 2, 16, 128], BF16)
            car, B_car = sb(st, tag + "car", [128, 2, 2, 16], F32)
            c1, B_c1 = sb(st, tag + "c1", [128, 16], F32)
            c2, B_c2 = sb(st, tag + "c2", [128, 16], F32)
            c3, B_c3 = sb(st, tag + "c3", [128, 16], F32)
            S.op("dve", lambda e: e.memset(car[:], 0.0), writes=[B_car])
            for j in range(nsb):
                sf, sbk = f_order[j], b_order[j]
                is_own = own(j)
                tf, B_tf = tfs[0]
                tb, B_tb = tfs[1]
                S.dma("sp", B_tf, [(tf[:], usd[sf * 1024:(sf + 1) * 1024, :].rearrange("(c i) h -> c i h", i=8))],
                      reads=[Bus], writes=[B_tf])
                S.dma("sp", B_tb, [(tb[:], usd[sbk * 1024:(sbk + 1) * 1024, :].rearrange("(c i) h -> c i h", i=8))],
                      reads=[Bus], writes=[B_tb])
                for (tsrc, B_tsrc, (tdst, B_tdst)) in ((tf, B_tf, tgs[0]), (tb, B_tb, tgs[1])):
                    S.op("act", lambda e, tsrc=tsrc, tdst=tdst: e.activation(
                        out=tdst[:].rearrange("p g (i h) -> p i g h", i=8),
                        in_=tsrc[:].rearrange("p i (g h) -> p i g h", g=32), func=AF.Copy), reads=[B_tsrc], writes=[B_tdst])
                for gh in range(2):
                    g0 = gh * 16
                    for (tsrc, B_tsrc, ud, B_ud) in ((tgs[0][0], tgs[0][1], uf_, B_uf_), (tgs[1][0], tgs[1][1], ub_, B_ub_)):
                        for gq in range(2):
                            pt, pb = PB.next()

                            def tru(e, tsrc=tsrc, gq=gq, pt=pt, g0=g0):
                                ins = None
                                for g8 in range(8):
                                    g = g0 + gq * 8 + g8
                                    ins = e.transpose(pt[:, g8 * 128:(g8 + 1) * 128], tsrc[:, g, :], identb[:])
                                return ins
                            S.op("pe", tru, reads=[B_tsrc, B_identb], writes=[pb])
                            S.op("act", lambda e, pt=pt, ud=ud, gq=gq: e.activation(
                                out=ud[:, gq * 8:(gq + 1) * 8, :], in_=pt[:].rearrange("p (g c) -> p g c", g=8),
                                func=AF.Copy), reads=[pb], writes=[B_ud])
                    if is_own:
                        S.dma("pool", B_uf_, [(U_sc[Ubase + sf, gh], uf_[:].rearrange("p g c -> p (g c)"))],
                              reads=[B_uf_], writes=[B_Usc])
                    for gq in range(4):
                        ptr_, pbr = PF.next()
                        pti_, pbi = PF.next()

                        def zmm(e, gq=gq, ptr_=ptr_, pti_=pti_, g0=g0):
                            ins = None
                            for g4 in range(4):
                                gl = gq * 4 + g4
                                g = g0 + gl
                                for (pt, kf, kb_) in ((ptr_, "rf", "rb"), (pti_, "if", "ib")):
                                    e.matmul(pt[:, g4 * 128:(g4 + 1) * 128], lhsT=wzt[kf][0][:, g, :], rhs=uf_[:, gl, :],
                                             start=True, stop=False)
                                    ins = e.matmul(pt[:, g4 * 128:(g4 + 1) * 128], lhsT=wzt[kb_][0][:, g, :],
                                                   rhs=ub_[:, gl, ::-1], start=False, stop=True)
                            return ins
                        S.op("pe", zmm, reads=[B_uf_, B_ub_] + [wzt[k][1] for k in wzt], writes=[pbr, pbi])
                        gs = slice(gq * 4, gq * 4 + 4)
                        ga = slice(g0 + gq * 4, g0 + gq * 4 + 4)
                        zr = ptr_[:].rearrange("p (g c) -> p g c", g=4)
                        zi = pti_[:].rearrange("p (g c) -> p g c", g=4)
                        cs_ = cosT[:, ga, 1:129]
                        sn_ = sinT[:, ga, 1:129]

                        def t2(out, a, b, op, reads, writes):
                            S.op("dve", lambda e: e.tensor_tensor(out=out, in0=a, in1=b, op=op), reads=reads, writes=writes)
                        t2(ztr[:, gs, :], zr, cs_, ALU.mult, [pbr, B_cosT], [B_ztr])
                        t2(sr[:, gs, :], zi, sn_, ALU.mult, [pbi, B_sinT], [B_sr])
                        t2(zti[:, gs, :], zi, cs_, ALU.mult, [pbi, B_cosT], [B_zti])
                        t2(si[:, gs, :], zr, sn_, ALU.mult, [pbr, B_sinT], [B_si])
                    S.op("dve", lambda e: e.tensor_tensor(out=ztr[:], in0=ztr[:], in1=sr[:], op=ALU.add),
                         reads=[B_ztr, B_sr], writes=[B_ztr])
                    S.op("dve", lambda e: e.tensor_tensor(out=zti[:], in0=zti[:], in1=si[:], op=ALU.subtract),
                         reads=[B_zti, B_si], writes=[B_zti])
                    rg = rho[:, g0:g0 + 16]
                    if use_mask:
                        S.op("dve", lambda e, gh=gh, j=j: e.tensor_scalar(
                            out=car[:, gh].rearrange("p r g -> p (r g)"), in0=car[:, gh].rearrange("p r g -> p (r g)"),
                            scalar1=maskt[:, j:j + 1], scalar2=None, op0=ALU.mult),
                            reads=[B_car, B_maskt], writes=[B_car])
                    for (ri, zt_, B_zt) in ((0, ztr, B_ztr), (1, zti, B_zti)):
                        S.op("dve", lambda e, ri=ri, gh=gh: e.tensor_tensor(out=c1[:], in0=car[:, gh, ri, :], in1=rg, op=ALU.mult),
                             reads=[B_car, B_rho], writes=[B_c1])
                        S.op("dve", lambda e, zt_=zt_: e.tensor_tensor(out=zt_[:, :, 0], in0=zt_[:, :, 0], in1=c1[:], op=ALU.add),
                             reads=[B_c1, B_zt], writes=[B_zt])
                    if is_own:
                        for ri in range(2):
                            S.op("dve", lambda e, ri=ri, gh=gh: e.tensor_copy(out=dd[:, ri, :, 0], in_=car[:, gh, ri, :]),
                                 reads=[B_car], writes=[B_dd])
                    decv = dec[:, g0:g0 + 16, :].rearrange("p g c -> p (g c)")
                    for (zt_, B_zt, s_, B_s) in ((ztr, B_ztr, sr, B_sr), (zti, B_zti, si, B_si)):
                        S.op("dve", lambda e, zt_=zt_, s_=s_: e.tensor_tensor_scan(
                            out=s_[:].rearrange("p g c -> p (g c)"), data0=decv,
                            data1=zt_[:].rearrange("p g c -> p (g c)"), initial=0.0, op0=ALU.mult, op1=ALU.add),
                            reads=[B_zt, B_dec], writes=[B_s])
                    cl = cosT[:, g0:g0 + 16, 128]
                    sl_ = sinT[:, g0:g0 + 16, 128]

                    def t3(out, a, b, op, reads, writes):
                        S.op("dve", lambda e: e.tensor_tensor(out=out, in0=a, in1=b, op=op), reads=reads, writes=writes)
                    t3(c1[:], sr[:, :, 127], cl, ALU.mult, [B_sr, B_cosT], [B_c1])
                    t3(c2[:], si[:, :, 127], sl_, ALU.mult, [B_si, B_sinT], [B_c2])
                    t3(c3[:], sr[:, :, 127], sl_, ALU.mult, [B_sr, B_sinT], [B_c3])
                    t3(car[:, gh, 0, :], c1[:], c2[:], ALU.subtract, [B_c1, B_c2], [B_car])
                    t3(c1[:], si[:, :, 127], cl, ALU.mult, [B_si, B_cosT], [B_c1])
                    t3(car[:, gh, 1, :], c3[:], c1[:], ALU.add, [B_c3, B_c1], [B_car])
                    if is_own:
                        ga = slice(g0, g0 + 16)
                        cs_ = cosT[:, ga, 1:128]
                        sn_ = sinT[:, ga, 1:128]
                        t3(ztr[:, :, 0:127], sr[:, :, 0:127], cs_, ALU.mult, [B_sr, B_cosT], [B_ztr])
                        S.op("pool", lambda e: e.tensor_tensor(out=zti[:, :, 0:127], in0=si[:, :, 0:127], in1=sn_, op=ALU.mult),
                             reads=[B_si, B_sinT], writes=[B_zti])
                        t3(dd[:, 0, :, 1:128], ztr[:, :, 0:127], zti[:, :, 0:127], ALU.subtract, [B_ztr, B_zti], [B_dd])
                        t3(ztr[:, :, 0:127], sr[:, :, 0:127], sn_, ALU.mult, [B_sr, B_sinT, B_dd], [B_ztr])
                        S.op("pool", lambda e: e.tensor_tensor(out=zti[:, :, 0:127], in0=si[:, :, 0:127], in1=cs_, op=ALU.mult),
                             reads=[B_si, B_cosT, B_dd], writes=[B_zti])
                        t3(dd[:, 1, :, 1:128], ztr[:, :, 0:127], zti[:, :, 0:127], ALU.add, [B_ztr, B_zti], [B_dd])
                        S.dma("pool", B_dd, [(XDd[own(j, True), gh], dd[:].rearrange("p r g c -> p (r g c)"))],
                              reads=[B_dd], writes=[BXD])

    def ssm_pass2(tag, nown, XDd, BXD, slot_f, slot_b, Ubase, ysd, Bys):
        with ExitStack() as st:
            ut, B_ut = sb(st, tag + "u", [128, 16, 128], BF16)
            xs_, B_xs = sb(st, tag + "x", [128, 2, 16, 128], BF16)
            yg, B_yg = sb(st, tag + "yg", [128, 32, 128], BF16)
            tts = [sb(st, tag + "tt%d" % i, [128, 8, 512], BF16) for i in range(2)]
            TT = Ring(tts)
            for s in range(nown):
                for gh in range(2):
                    S.dma("sp", B_ut, [(ut[:].rearrange("p g c -> p (g c)"), U_sc[Ubase + s, gh])],
                          reads=[B_Usc], writes=[B_ut])
                    S.dma("sp", B_xs, [(xs_[0:64].rearrange("p r g c -> p (r g c)"), XDd[slot_f(s), gh, 0:64, :]),
                                       (xs_[64:128].rearrange("p r g c -> p (r g c)"), XDd[slot_b(s), gh, 64:128, :])],
                          reads=[BXD], writes=[B_xs])
                    for gq in range(4):
                        pt, pb = PF.next()

                        def ymm(e, gq=gq, pt=pt, gh=gh):
                            ins = None
                            for g4 in range(4):
                                gl = gq * 4 + g4
                                g = gh * 16 + gl
                                o = pt[:, g4 * 128:(g4 + 1) * 128]
                                e.matmul(o, lhsT=kloc[:, g, :], rhs=ut[:, gl, :], start=True, stop=False)
                                e.matmul(o, lhsT=woRf[:, g, :], rhs=xs_[:, 0, gl, :], start=False, stop=False)
                                e.matmul(o, lhsT=woIf[:, g, :], rhs=xs_[:, 1, gl, :], start=False, stop=False)
                                e.matmul(o, lhsT=woRb[:, g, :], rhs=xs_[:, 0, gl, ::-1], start=False, stop=False)
                                ins = e.matmul(o, lhsT=woIb[:, g, :], rhs=xs_[:, 1, gl, ::-1], start=False, stop=True)
                            return ins
                        S.op("pe", ymm, reads=[B_ut, B_xs, B_kloc, B_woRf, B_woRb, B_woIf, B_woIb], writes=[pb])
                        S.op("act", lambda e, pt=pt, gq=gq, gh=gh: e.activation(
                            out=yg[:, gh * 16 + gq * 4:gh * 16 + gq * 4 + 4, :],
                            in_=pt[:].rearrange("p (g c) -> p g c", g=4), func=AF.Copy), reads=[pb], writes=[B_yg])
                tt_, B_tt = TT.next()
                for gq in range(4):
                    pt, pb = PB.next()

                    def try_(e, gq=gq, pt=pt):
                        ins = None
                        for g8 in range(8):
                            ins = e.transpose(pt[:, g8 * 128:(g8 + 1) * 128], yg[:, gq * 8 + g8, :], identb[:])
                        return ins
                    S.op("pe", try_, reads=[B_yg, B_identb], writes=[pb])
                    src = pt[:].rearrange("p (g j h) -> p g j h", g=8, j=8)
                    dst = tt_[:, :, gq * 128:(gq + 1) * 128].rearrange("p j (g h) -> p g j h", g=8)
                    S.op("dve", lambda e, src=src, dst=dst: e.tensor_copy(out=dst, in_=src), reads=[pb], writes=[B_tt])
                S.dma("pool", B_tt, [(ysd[s * 1024:(s + 1) * 1024, :].rearrange("(c j) h -> c j h", j=8), tt_[:])],
                      reads=[B_tt], writes=[Bys])

    def own_p(j, slot=False):
        return j if slot else True

    def own_s(j, slot=False):
        return (j - 14) if slot else (j >= 14)

    if "S" in phases:
      ssm_pass1("s1p_", us_p, B_usp, 8, list(range(8)), list(range(7, -1, -1)), own_p, XD_p, B_XDp, 0, False)
      S.barrier()
      ssm_pass2("s2p_", 8, XD_p, B_XDp, lambda s: s, lambda s: 7 - s, 0, ys_p, B_ysp)
      S.barrier()
      ssm_pass1("s1s_", us_s, B_uss, 16, [(2 + j) % 16 for j in range(16)], list(range(15, -1, -1)), own_s, XD_s, B_XDs, 8, True)
      S.barrier()
      ssm_pass2("s2s_", 2, XD_s, B_XDs, lambda s: s, lambda s: 1 - s, 8, ys_s, B_yss)
      S.barrier()
    ph_b.close()
    if stop == "S":
        S.finish()
        return nc

    ph_c = ExitStack()
    g2t, B_g2t = sb(ph_c, "g2t", [128, 1024], F32)
    g4t, B_g4t = sb(ph_c, "g4t", [128, 1024], F32)
    S.dma("sp", B_g2t, [(g2t[:], gains[1:2, :].broadcast_to([128, 1024]))], writes=[B_g2t])
    S.dma("sp", B_g4t, [(g4t[:], gains[3:4, :].broadcast_to([128, 1024]))], writes=[B_g4t])
    wfp, B_wfp = sb(ph_c, "wfp", [128, 8, 1024], BF16)
    wv, B_wv = sb(ph_c, "wv", [128, 4, 1024], BF16)
    wgg, B_wgg = sb(ph_c, "wgg", [128, 4, 1024], BF16)
    wo_, B_wo = sb(ph_c, "wo", [128, 8, 1024], BF16)
    with ExitStack() as st:
        ld = [sb(st, "cw_ld%d" % i, [128, 1024], F32) for i in range(3)]
        LD = Ring(ld)
        ccs32, B_ccs32 = sb(st, "ccs32", [128, 256], F32)
        ccs, B_ccs = sb(st, "ccs", [128, 256], BF16)
        ltb, B_ltb = sb(st, "cw_ltb", [128, 1024], BF16)
        S.dma("sp", B_ccs32, [(ccs32[:], c_cs[:, :])], writes=[B_ccs32])
        S.op("dve", lambda e: e.tensor_copy(out=ccs[:], in_=ccs32[:]), reads=[B_ccs32], writes=[B_ccs])
        for (wsrc, nkc, dst, B_dst) in ((w_gv, 4, wv, B_wv), (w_gg, 4, wgg, B_wgg), (w_o, 8, wo_, B_wo)):
            for kc in range(nkc):
                lt, lb = LD.next()
                S.dma("sp", lb, [(lt[:], wsrc[kc * 128:(kc + 1) * 128, :])], writes=[lb])
                S.op("act", lambda e, lt=lt, dst=dst, kc=kc: e.activation(out=dst[:, kc, :], in_=lt[:], func=AF.Copy),
                     reads=[lb], writes=[B_dst])
        for g in range(4):
            lt, lb = LD.next()
            S.dma("sp", lb, [(lt[:], w_fn[g * 128:(g + 1) * 128, :])], writes=[lb])
            S.op("dve", lambda e, lt=lt: e.tensor_copy(out=ltb[:], in_=lt[:]), reads=[lb], writes=[B_ltb])
            for r in range(2):
                for hf in range(2):
                    pt, pb = PF.next()
                    S.op("pe", lambda e, pt=pt, lt=lt, r=r, hf=hf: e.matmul(
                        pt[:], lhsT=ccs[:, r * 128:(r + 1) * 128], rhs=ltb[:, hf * 512:(hf + 1) * 512], start=True, stop=True),
                        reads=[B_ltb, B_ccs], writes=[pb])
                    S.op("act", lambda e, pt=pt, g=g, r=r, hf=hf: e.activation(
                        out=wfp[:, g * 2 + r, hf * 512:(hf + 1) * 512], in_=pt[:], func=AF.Copy),
                        reads=[pb], writes=[B_wfp])

    S.barrier()
    if stop == "R":
        S.finish()
        return nc
    with ExitStack() as st:
        xt, B_xt = sb(st, "c_xt", [128, 4, 1024], F32)
        XK = Ring([sb(st, "c_xk%d" % i, [128, 1024], F32) for i in range(2)])
        hbF, B_hbF = sb(st, "c_hb", [128, 4, 1024], BF16)
        hbH, B_hbH = hbF, B_hbF
        hT1, B_hT1 = sb(st, "c_hT1", [128, 8, 512], BF16)
        hT3, B_hT3 = sb(st, "c_hT3", [128, 8, 512], BF16)
        sg, B_sg = sb(st, "c_sg", [128, 16, 512], BF16)
        vt, B_vt = sb(st, "c_vt", [128, 8, 512], BF16)
        mg, B_mg = sb(st, "c_mg", [128, 8, 512], BF16)
        yt, B_yt = sb(st, "c_yt", [128, 4, 512], BF16)
        zT, B_zT = sb(st, "c_zT", [128, 4, 512], BF16)
        zp, B_zp = sb(st, "c_zp", [128, 512], F32)
        zq, B_zq = sb(st, "c_zq", [128, 512], F32)
        af, B_af = sb(st, "c_af", [128, NFC, 512], BF16)
        SGT = Ring([sb(st, "c_sgt%d" % i, [128, 512], BF16) for i in range(2)])
        TM = Ring([sb(st, "c_tm%d" % i, [128, 512], F32) for i in range(2)])
        WST = Ring([sb(st, "c_ws%d" % i, [128, 8, 128], BF16) for i in range(8)])
        WDS = Ring([sb(st, "c_wd%d" % i, [128, 1024], BF16) for i in range(2)])
        ss, B_ss = sb(st, "c_ss", [128, 8], F32)
        ssF, B_ssF = sb(st, "c_ssF", [128, 4], F32)
        rstd, B_rstd = sb(st, "c_rstd", [128, 4], F32)
        rstdF, B_rstdF = sb(st, "c_rstdF", [128, 4], F32)
        junk, B_junk = sb(st, "c_junk", [128, 1024], BF16)
        OB = Ring([sb(st, "c_ob%d" % i, [128, 1024], F32) for i in range(2)])

        def load_w(m):
            wt, wb = WST.next()
            S.dma("sp", wb, [(wt[:].rearrange("p a f -> p (a f)"), wcat[m].rearrange("p a f -> p (a f)"))],
                  reads=[B_wcat], writes=[wb])
            return wt, wb

        def fm_mm(pt, wt, rhs_tile, nk):
            def f(e):
                ins = None
                for kc in range(nk):
                    ins = e.matmul(pt[:], lhsT=wt[:, kc, :], rhs=rhs_tile[:, kc, :], start=(kc == 0), stop=(kc == nk - 1))
                return ins
            return f

        tiles = [(x_p, V_p, B_Vp, ys_p, B_ysp, y_p, t) for t in range(SP // 512)] + \
                [(x_s, V_s, B_Vs, ys_s, B_yss, y_s, t) for t in range(OWN // 512)]
        if "C" not in phases:
            tiles = []
        if "1" in phases:
            tiles = tiles[:2]

        def c_front_act(tile):
            (xsrc, Vd, BV, ysd, Bys, ydst, t) = tile
            for k in range(4):
                xk, B_xk = XK.next()
                S.dma("sp", B_xk, [(xk[:], xsrc[t * 512 + k * 128:t * 512 + (k + 1) * 128, :])], writes=[B_xk])
                S.op("act", lambda e, k=k, xk=xk: e.activation(out=junk[:], in_=xk[:], func=AF.Square,
                                                               accum_out=ssF[:, k:k + 1]),
                     reads=[B_xk], writes=[B_junk, B_ssF])
                S.op("dve", lambda e, k=k: e.tensor_copy(out=rstdF[:, k:k + 1], in_=ssF[:, k:k + 1]),
                     reads=[B_ssF], writes=[B_rstdF])
                rsq(rstdF[:, k:k + 1], B_rstdF)
                S.op("act", lambda e, k=k, xk=xk: e.activation(out=hbF[:, k, :], in_=xk[:], func=AF.Copy,
                                                               scale=rstdF[:, k:k + 1]),
                     reads=[B_xk, B_rstdF], writes=[B_hbF])

        def c_front_pe():
            transpose_hT(hbF, B_hbF, hT1, B_hT1)

        def c_gelu(tile):
            (xsrc, Vd, BV, ysd, Bys, ydst, t) = tile
            rows = slice(t * 512, (t + 1) * 512)
            S.dma("sp", B_yt, [(yt[:], ysd[rows, :].rearrange("(k p) c -> p k c", p=128))], reads=[Bys], writes=[B_yt])
            for cp in range(2):
                pt, pb = PB.next()

                def trz2(e, pt=pt, cp=cp):
                    ins = None
                    for c2 in range(2):
                        for k in range(4):
                            cc = cp * 2 + c2
                            ins = e.transpose(pt[:, (c2 * 4 + k) * 128:(c2 * 4 + k + 1) * 128],
                                              yt[:, k, cc * 128:(cc + 1) * 128], identb[:])
                    return ins
                S.op("pe", trz2, reads=[B_yt, B_identb], writes=[pb])
                for hh in range(2):
                    pv = pt[:, hh * 512:(hh + 1) * 512]
                    S.op("act", lambda e, pv=pv: e.activation(out=zp[:], in_=pv, func=AF.Square), reads=[pb], writes=[B_zp])
                    S.op("dve", lambda e: e.tensor_scalar(out=zp[:], in0=zp[:], scalar1=0.044715, scalar2=1.0,
                                                          op0=ALU.mult, op1=ALU.add), reads=[B_zp], writes=[B_zp])
                    S.op("dve", lambda e, pv=pv: e.tensor_tensor(out=zq[:], in0=pv, in1=zp[:], op=ALU.mult),
                         reads=[pb, B_zp], writes=[B_zq])
                    S.op("act", lambda e: e.activation(out=zp[:], in_=zq[:], func=AF.Sigmoid, scale=GELU_C),
                         reads=[B_zq], writes=[B_zp])
                    S.op("dve", lambda e, pv=pv, cp=cp, hh=hh: e.tensor_tensor(
                        out=zT[:, cp * 2 + hh, :], in0=pv, in1=zp[:], op=ALU.mult),
                        reads=[pb, B_zp], writes=[B_zT])

        def c_gates():
            for mc in range(16):
                wt, wb = load_w(mc)
                pt, pb = PF.next()
                S.op("pe", fm_mm(pt, wt, hT1, 8), reads=[wb, B_hT1], writes=[pb])
                S.op("act", lambda e, pt=pt, mc=mc: e.activation(out=sg[:, mc, :], in_=pt[:], func=AF.Sigmoid),
                     reads=[pb], writes=[B_sg])
        def c_branch_glu(tile):
            (xsrc, Vd, BV, ysd, Bys, ydst, t) = tile
            rows = slice(t * 512, (t + 1) * 512)
            S.dma("sp", B_vt, [(vt[:], Vd[:, rows].rearrange("(a p) t -> p a t", p=128))], reads=[BV], writes=[B_vt])
            for mc in range(8):
                pt, pb = PF.next()

                def bra(e, pt=pt, mc=mc):
                    ins = None
                    for kc in range(8):
                        ins = e.matmul(pt[:], lhsT=wfp[:, kc, mc * 128:(mc + 1) * 128], rhs=vt[:, kc, :],
                                       start=(kc == 0), stop=(kc == 7))
                    return ins
                S.op("pe", bra, reads=[B_wfp, B_vt], writes=[pb])
                S.op("dve", lambda e, pt=pt, mc=mc: e.tensor_tensor(out=mg[:, mc, :], in0=pt[:], in1=sg[:, mc, :], op=ALU.mult),
                     reads=[pb, B_sg], writes=[B_mg])
            for mc in range(8):
                ptv, pbv = PF.next()
                ptg, pbg = PF.next()

                def glu(e, ptv=ptv, ptg=ptg, mc=mc):
                    ins = None
                    for kc in range(4):
                        e.matmul(ptv[:], lhsT=wv[:, kc, mc * 128:(mc + 1) * 128], rhs=zT[:, kc, :], start=(kc == 0), stop=(kc == 3))
                    for kc in range(4):
                        ins = e.matmul(ptg[:], lhsT=wgg[:, kc, mc * 128:(mc + 1) * 128], rhs=zT[:, kc, :], start=(kc == 0), stop=(kc == 3))
                    return ins
                S.op("pe", glu, reads=[B_wv, B_wgg, B_zT], writes=[pbv, pbg])
                s_, B_s = SGT.next()
                t_, B_t = TM.next()
                S.op("act", lambda e, ptg=ptg, s_=s_: e.activation(out=s_[:], in_=ptg[:], func=AF.Sigmoid),
                     reads=[pbg], writes=[B_s])
                S.op("dve", lambda e, ptv=ptv, s_=s_, t_=t_: e.tensor_tensor(out=t_[:], in0=ptv[:], in1=s_[:], op=ALU.mult),
                     reads=[pbv, B_s], writes=[B_t])
                S.op("dve", lambda e, t_=t_, mc=mc: e.tensor_tensor(out=t_[:], in0=t_[:], in1=sg[:, 8 + mc, :], op=ALU.mult),
                     reads=[B_t, B_sg], writes=[B_t])
                S.op("pool", lambda e, t_=t_, mc=mc: e.tensor_tensor(out=mg[:, mc, :], in0=t_[:], in1=mg[:, mc, :], op=ALU.add),
                     reads=[B_t, B_mg], writes=[B_mg])
        def c_wout(tile):
            (xsrc, Vd, BV, ysd, Bys, ydst, t) = tile
            rows = slice(t * 512, (t + 1) * 512)
            S.dma("sp", B_xt, [(xt[:], xsrc[rows, :].rearrange("(k p) d -> p k d", p=128))], writes=[B_xt])
            for k in range(4):
                pts = []
                for hf in range(2):
                    pt, pb = PF.next()

                    def wom(e, pt=pt, k=k, hf=hf):
                        ins = None
                        for dc in range(8):
                            ins = e.matmul(pt[:], lhsT=mg[:, dc, k * 128:(k + 1) * 128], rhs=wo_[:, dc, hf * 512:(hf + 1) * 512],
                                           start=(dc == 0), stop=(dc == 7))
                        return ins
                    S.op("pe", wom, reads=[B_mg, B_wo], writes=[pb])
                    S.op("act", lambda e, pt=pt, k=k, hf=hf: e.activation(out=junk[:, 0:512], in_=pt[:], func=AF.Square,
                                                                          accum_out=ss[:, k * 2 + hf:k * 2 + hf + 1]),
                         reads=[pb], writes=[B_junk, B_ss])
                    pts.append((pt, pb))
                S.op("dve", lambda e, k=k: e.tensor_tensor(out=rstd[:, k:k + 1], in0=ss[:, 2 * k:2 * k + 1], in1=ss[:, 2 * k + 1:2 * k + 2], op=ALU.add),
                     reads=[B_ss], writes=[B_rstd])
                rsq(rstd[:, k:k + 1], B_rstd)
                for hf in range(2):
                    pt, pb = pts[hf]
                    t_, B_t = TM.next()
                    S.op("dve", lambda e, pt=pt, k=k, hf=hf, t_=t_: e.scalar_tensor_tensor(
                        out=t_[:], in0=pt[:], scalar=rstd[:, k:k + 1], in1=g2t[:, hf * 512:(hf + 1) * 512], op0=ALU.mult, op1=ALU.mult),
                        reads=[pb, B_rstd, B_g2t], writes=[B_t])
                    S.op("pool", lambda e, k=k, hf=hf, t_=t_: e.tensor_tensor(
                        out=xt[:, k, hf * 512:(hf + 1) * 512], in0=xt[:, k, hf * 512:(hf + 1) * 512], in1=t_[:], op=ALU.add),
                        reads=[B_t, B_xt], writes=[B_xt])
        def c_rms3_act():
            for k in range(4):
                S.op("dve", lambda e, k=k: e.scalar_tensor_tensor(out=junk[:], in0=xt[:, k, :], scalar=1.0, in1=xt[:, k, :],
                                                                  op0=ALU.mult, op1=ALU.mult, accum_out=ss[:, k:k + 1]),
                     reads=[B_xt], writes=[B_junk, B_ss])
            S.op("dve", lambda e: e.tensor_copy(out=rstd[:, 0:4], in_=ss[:, 0:4]), reads=[B_ss], writes=[B_rstd])
            rsq(rstd[:, 0:4], B_rstd)
            for k in range(4):
                S.op("dve", lambda e, k=k: e.tensor_scalar(out=hbH[:, k, :], in0=xt[:, k, :], scalar1=rstd[:, k:k + 1],
                                                           scalar2=None, op0=ALU.mult),
                     reads=[B_xt, B_rstd], writes=[B_hbH])

        def c_hT3_pe():
            transpose_hT(hbH, B_hbH, hT3, B_hT3)

        def c_ffn_up():
            for fc in range(NFC):
                wtg, wbg = load_w(16 + fc)
                wtu, wbu = load_w(38 + fc)
                ptg, pbg = PF.next()
                ptu, pbu = PF.next()
                S.op("pe", fm_mm(ptg, wtg, hT3, 8), reads=[wbg, B_hT3], writes=[pbg])
                S.op("pe", fm_mm(ptu, wtu, hT3, 8), reads=[wbu, B_hT3], writes=[pbu])
                s_, B_s = SGT.next()
                S.op("act", lambda e, ptg=ptg, s_=s_: e.activation(out=s_[:], in_=ptg[:], func=AF.Silu), reads=[pbg], writes=[B_s])
                S.op("dve", lambda e, ptu=ptu, s_=s_, fc=fc: e.tensor_tensor(out=af[:, fc, :], in0=ptu[:], in1=s_[:], op=ALU.mult),
                     reads=[pbu, B_s], writes=[B_af])

        def c_down_mm(kp):
            grp = []
            for kk in range(2):
                for hf in range(2):
                    grp.append((kp * 2 + kk, hf) + PF.next())
            for fc in range(NFC):
                wd, wdb = WDS.next()
                S.dma("sp", wdb, [(wd[:], wd_sc[fc * 128:(fc + 1) * 128, :])], reads=[B_wd], writes=[wdb])

                def dmm(e, grp=grp, wd=wd, fc=fc):
                    ins = None
                    for (k, hf, pt, pb) in grp:
                        ins = e.matmul(pt[:], lhsT=af[:, fc, k * 128:(k + 1) * 128], rhs=wd[:, hf * 512:(hf + 1) * 512],
                                       start=(fc == 0), stop=(fc == NFC - 1))
                    return ins
                S.op("pe", dmm, reads=[wdb, B_af], writes=[g_[3] for g_ in grp])
            return grp

        def c_down_epi(tile, kp, grp):
            (xsrc, Vd, BV, ysd, Bys, ydst, t) = tile
            for (k, hf, pt, pb) in grp:
                S.op("act", lambda e, pt=pt, k=k, hf=hf: e.activation(out=junk[:, 0:512], in_=pt[:], func=AF.Square,
                                                                      accum_out=ss[:, k * 2 + hf:k * 2 + hf + 1]),
                     reads=[pb], writes=[B_junk, B_ss])
            for kk in range(2):
                k = kp * 2 + kk
                S.op("dve", lambda e, k=k: e.tensor_tensor(out=rstd[:, k:k + 1], in0=ss[:, 2 * k:2 * k + 1], in1=ss[:, 2 * k + 1:2 * k + 2], op=ALU.add),
                     reads=[B_ss], writes=[B_rstd])
                rsq(rstd[:, k:k + 1], B_rstd)
                o_, B_o = OB.next()
                for hf in range(2):
                    pt, pb = [(g_[2], g_[3]) for g_ in grp if g_[0] == k and g_[1] == hf][0]
                    t_, B_t = TM.next()
                    S.op("dve", lambda e, pt=pt, k=k, hf=hf, t_=t_: e.scalar_tensor_tensor(
                        out=t_[:], in0=pt[:], scalar=rstd[:, k:k + 1], in1=g4t[:, hf * 512:(hf + 1) * 512], op0=ALU.mult, op1=ALU.mult),
                        reads=[pb, B_rstd, B_g4t], writes=[B_t])
                    S.op("pool", lambda e, k=k, hf=hf, t_=t_, o_=o_: e.tensor_tensor(
                        out=o_[:, hf * 512:(hf + 1) * 512], in0=xt[:, k, hf * 512:(hf + 1) * 512], in1=t_[:], op=ALU.add),
                        reads=[B_t, B_xt], writes=[B_o])
                S.dma("pool", B_o, [(ydst[t * 512 + k * 128:t * 512 + (k + 1) * 128, :], o_[:])],
                      reads=[B_o], writes=[B_dummy_out], final=True)

        if tiles:
            c_front_act(tiles[0])
            c_front_pe()
            c_gelu(tiles[0])
            c_gates()
        for ti, tile in enumerate(tiles):
            nxt = tiles[ti + 1] if ti + 1 < len(tiles) else None
            c_branch_glu(tile)
            if nxt is not None:
                c_front_act(nxt)
            c_wout(tile)
            if nxt is not None:
                c_front_pe()
            c_rms3_act()
            if nxt is not None:
                c_gelu(nxt)
                c_gates()
            c_hT3_pe()
            c_ffn_up()
            g0_ = c_down_mm(0)
            c_down_epi(tile, 0, g0_)
            g1_ = c_down_mm(1)
            c_down_epi(tile, 1, g1_)
    ph_c.close()
    S.finish()
    es.close()
    return nc


def _consts():
    bf = ml_dtypes.bfloat16
    c = {}
    c["c_identb"] = np.eye(128, dtype=np.float32).astype(bf)
    c["c_identf"] = np.eye(128, dtype=np.float32)
    n1 = np.arange(128)[:, None].astype(np.float64)
    k1 = np.arange(128)[None, :].astype(np.float64)
    ang = TWO_PI * n1 * k1 / 128.0
    f1 = np.zeros((128, 2, 128), np.float64)
    for kh in range(2):
        f1[:, kh, 0:64] = np.cos(ang[:, kh * 64:(kh + 1) * 64])
        f1[:, kh, 64:128] = np.sin(ang[:, kh * 64:(kh + 1) * 64])
    c["c_f1"] = f1.astype(np.float32).astype(bf)
    n2 = np.arange(64)[:, None, None].astype(np.float64)
    kk = (np.arange(128)[None, :, None] + 128 * np.arange(64)[None, None, :]).astype(np.float64)
    ph = -TWO_PI * n2 * kk / SP
    gr, gi = np.cos(ph) / math.sqrt(SP), np.sin(ph) / math.sqrt(SP)
    c["c_gp"] = np.concatenate([gr, gi, -gr], axis=2).astype(np.float32).astype(bf)
    ch = np.arange(128)[:, None].astype(np.float64)
    ch2 = np.arange(128)[None, :].astype(np.float64)
    a2 = TWO_PI * ch * ch2 / 128.0
    c["c_cs"] = (np.concatenate([np.cos(a2), np.sin(a2)], axis=1) / math.sqrt(128.0)).astype(np.float32)
    c["c_kv17"] = np.tile(np.arange(-8, 9, dtype=np.float32)[None, :], (128, 1))
    c["c_kv129"] = np.tile(np.arange(0, 129, dtype=np.float32)[None, :], (128, 1))
    ii = (np.arange(128) // 16)[:, None]
    jj = (np.arange(128) // 16)[None, :]
    c["c_mf"] = (ii <= jj).astype(np.float32)
    c["c_mb"] = (ii >= jj).astype(np.float32)
    return c


def _core_consts(q):
    bf = ml_dtypes.bfloat16
    c = {}
    s = OWN * q
    n2 = np.arange(128)[:, None, None].astype(np.float64)
    kk = (np.arange(128)[None, :, None] + 128 * (16 * q + np.arange(16))[None, None, :]).astype(np.float64)
    ph = -TWO_PI * ((n2 + s) * kk % SS) / SS
    gr, gi = np.cos(ph) / math.sqrt(SS), np.sin(ph) / math.sqrt(SS)
    c["c_gs"] = np.concatenate([gr, gi, -gr], axis=2).astype(np.float32).astype(bf)
    mk = np.ones((128, 16), np.float32)
    for j in range(16):
        sf = (2 + j) % 16
        sbk = 15 - j
        if sf == (16 - 2 * q) % 16:
            mk[0:64, j] = 0.0
        if sbk == 15 - 2 * q:
            mk[64:128, j] = 0.0
    c["c_mask"] = mk
    return c


_NC_CACHE = {}


def kernel(x_prompt, x_sample, norm_mix_pre, norm_mix_post, norm_ffn_pre, norm_ffn_post, w_in,
           w_fnet_out, lam_re, lam_im, log_dt, b_re, b_im, c_re, c_im, d_skip, w_glu_val,
           w_glu_gate, w_out, w_ffn_gate, w_ffn_up, w_ffn_down):
    f32 = np.float32
    if "nc" not in _NC_CACHE:
        _NC_CACHE["nc"] = build_program()
    nc = _NC_CACHE["nc"]
    shared = {
        "gains": np.ascontiguousarray(np.stack([np.asarray(norm_mix_pre, f32)[0], np.asarray(norm_mix_post, f32)[0],
                                                np.asarray(norm_ffn_pre, f32)[0], np.asarray(norm_ffn_post, f32)[0]])),
        "w_in": np.ascontiguousarray(np.asarray(w_in, f32)[0]),
        "w_fn": np.ascontiguousarray(np.asarray(w_fnet_out, f32)[0]),
        "lam_re": np.ascontiguousarray(np.asarray(lam_re, f32)[0]),
        "lam_im": np.ascontiguousarray(np.asarray(lam_im, f32)[0]),
        "log_dt": np.ascontiguousarray(np.asarray(log_dt, f32)[0]),
        "b_re": np.ascontiguousarray(np.asarray(b_re, f32)[0]),
        "b_im": np.ascontiguousarray(np.asarray(b_im, f32)[0]),
        "c_re": np.ascontiguousarray(np.asarray(c_re, f32)[0]),
        "c_im": np.ascontiguousarray(np.asarray(c_im, f32)[0]),
        "d_skip": np.ascontiguousarray(np.asarray(d_skip, f32)[0]),
        "w_gv": np.ascontiguousarray(np.asarray(w_glu_val, f32)[0]),
        "w_gg": np.ascontiguousarray(np.asarray(w_glu_gate, f32)[0]),
        "w_o": np.ascontiguousarray(np.asarray(w_out, f32)[0]),
        "w_fg": np.ascontiguousarray(np.asarray(w_ffn_gate, f32)[0]),
        "w_fu": np.ascontiguousarray(np.asarray(w_ffn_up, f32)[0]),
        "w_fd": np.ascontiguousarray(np.asarray(w_ffn_down, f32)[0]),
    }
    shared.update(_consts())
    xp = np.asarray(x_prompt, f32)
    xs = np.asarray(x_sample, f32)[0]
    in_maps = []
    for q in range(8):
        m = dict(shared)
        m["x_p"] = np.ascontiguousarray(xp[q])
        m["x_s"] = np.ascontiguousarray(np.roll(xs, -OWN * q, axis=0))
        m.update(_core_consts(q))
        in_maps.append(m)
    res = run_bass_kernel_spmd(nc, in_maps, core_ids=list(range(8)))
    yp = np.stack([np.asarray(res.results[q]["y_p"], f32) for q in range(8)], axis=0)
    ysm = np.concatenate([np.asarray(res.results[q]["y_s"], f32) for q in range(8)], axis=0)[None]
    return (yp, ysm)
```

```python
import math
from contextlib import ExitStack

import numpy as np
import ml_dtypes

import concourse.bass as bass
import concourse.mybir as mybir
from concourse.bass_utils import run_bass_kernel_spmd

F32 = mybir.dt.float32
BF16 = mybir.dt.bfloat16
AF = mybir.ActivationFunctionType
ALU = mybir.AluOpType

D = 1024
SP = 8192
SS = 16384
OWN = 2048
FF = 2816
NFC = FF // 128
EPS = 1e-6
TWO_PI = 2.0 * math.pi
GELU_C = 2.0 * math.sqrt(2.0 / math.pi)


class Buf:
    def __init__(self, name):
        self.name = name
        self.lw = None
        self.rd = []
        self.dsem = None
        self.dcnt = 0


class Sched:
    def __init__(self, nc, es):
        self.nc = nc
        self.es = es
        self.eng = {"pe": nc.tensor, "act": nc.scalar, "dve": nc.vector, "pool": nc.gpsimd, "sp": nc.sync}
        self.sem = {k: es.enter_context(nc.semaphore("s_" + k)) for k in self.eng}
        self.cnt = {k: 0 for k in self.eng}
        self.seen = {k: {} for k in self.eng}
        self.final = []
        self.nsem = 0
        self.dma_bufs = []

    def _need(self, e, deps):
        best = {}
        for (h, key, val) in deps:
            if key not in best or best[key][1] < val:
                best[key] = (h, val)
        for key, (h, val) in best.items():
            if self.seen[e].get(key, 0) >= val:
                continue
            self.eng[e].wait_ge(h, val)
            self.seen[e][key] = val

    @staticmethod
    def _deps(reads, writes):
        deps = []
        for b in reads:
            if b.lw is not None:
                deps.append(b.lw)
        for b in writes:
            if b.lw is not None:
                deps.append(b.lw)
            deps.extend(b.rd)
        return deps

    def op(self, e, fn, reads=(), writes=()):
        deps = self._deps(reads, writes)
        if e == "pe":
            deps = [d for d in deps if d[1] != "pe"]
        self._need(e, deps)
        ins = fn(self.eng[e])
        self.cnt[e] += 1
        ins.then_inc(self.sem[e], 1)
        tok = (self.sem[e], e, self.cnt[e])
        for b in writes:
            b.lw = tok
            b.rd = []
        for b in reads:
            b.rd.append(tok)

    def dma(self, q, owner, pairs, reads=(), writes=(), final=False, **kw):
        self._need(q, self._deps(reads, writes))
        if owner.dsem is None:
            owner.dsem = self.es.enter_context(self.nc.semaphore("d%d_%s" % (self.nsem, owner.name)))
            self.nsem += 1
            self.dma_bufs.append(owner)
        for (o, i) in pairs:
            self.eng[q].dma_start(out=o, in_=i, **kw).then_inc(owner.dsem, 16)
            owner.dcnt += 16
        tok = (owner.dsem, "d_" + owner.name, owner.dcnt)
        for b in writes:
            b.lw = tok
            b.rd = []
        for b in reads:
            b.rd.append(tok)
        if final:
            self.final.append(tok)

    def barrier(self):
        toks = [(self.sem[e], e, self.cnt[e]) for e in self.eng if self.cnt[e] > 0]
        toks += [(b.dsem, "d_" + b.name, b.dcnt) for b in self.dma_bufs if b.dcnt > 0]
        for e in self.eng:
            self._need(e, toks)

    def finish(self):
        self._need("sp", self.final)
        toks = [(self.sem[e], e, self.cnt[e]) for e in ("pe", "act", "dve", "pool") if self.cnt[e] > 0]
        self._need("sp", toks)


class Ring:
    def __init__(self, items):
        self.items = items
        self.i = 0

    def next(self):
        it = self.items[self.i % len(self.items)]
        self.i += 1
        return it


def build_program(debug=False, phases="AFSC", stop=None):
    nc = bass.Bass("TRN2", target_bir_lowering=False)
    DBG = "ExternalOutput" if debug else "Internal"
    es = ExitStack()
    S = Sched(nc, es)

    def dram(name, shape, dt, kind):
        return nc.dram_tensor(name, list(shape), dt, kind=kind).ap()

    x_p = dram("x_p", [SP, D], F32, "ExternalInput")
    x_s = dram("x_s", [SS, D], F32, "ExternalInput")
    y_p = dram("y_p", [SP, D], F32, "ExternalOutput")
    y_s = dram("y_s", [OWN, D], F32, "ExternalOutput")
    gains = dram("gains", [4, D], F32, "ExternalInput")
    w_in = dram("w_in", [D, 3072], F32, "ExternalInput")
    w_fn = dram("w_fn", [512, D], F32, "ExternalInput")
    lam_re = dram("lam_re", [2, 32, 64], F32, "ExternalInput")
    lam_im = dram("lam_im", [2, 32, 64], F32, "ExternalInput")
    log_dt = dram("log_dt", [2, 32], F32, "ExternalInput")
    b_re = dram("b_re", [2, 32, 64, 16], F32, "ExternalInput")
    b_im = dram("b_im", [2, 32, 64, 16], F32, "ExternalInput")
    c_re = dram("c_re", [2, 32, 16, 64], F32, "ExternalInput")
    c_im = dram("c_im", [2, 32, 16, 64], F32, "ExternalInput")
    d_skip = dram("d_skip", [512], F32, "ExternalInput")
    w_gv = dram("w_gv", [512, D], F32, "ExternalInput")
    w_gg = dram("w_gg", [512, D], F32, "ExternalInput")
    w_o = dram("w_o", [D, D], F32, "ExternalInput")
    w_fg = dram("w_fg", [D, FF], F32, "ExternalInput")
    w_fu = dram("w_fu", [D, FF], F32, "ExternalInput")
    w_fd = dram("w_fd", [FF, D], F32, "ExternalInput")
    c_identb = dram("c_identb", [128, 128], BF16, "ExternalInput")
    c_identf = dram("c_identf", [128, 128], F32, "ExternalInput")
    c_f1 = dram("c_f1", [128, 2, 128], BF16, "ExternalInput")
    c_gp = dram("c_gp", [64, 128, 192], BF16, "ExternalInput")
    c_gs = dram("c_gs", [128, 128, 48], BF16, "ExternalInput")
    c_cs = dram("c_cs", [128, 256], F32, "ExternalInput")
    c_kv17 = dram("c_kv17", [128, 17], F32, "ExternalInput")
    c_kv129 = dram("c_kv129", [128, 129], F32, "ExternalInput")
    c_mf = dram("c_mf", [128, 128], F32, "ExternalInput")
    c_mb = dram("c_mb", [128, 128], F32, "ExternalInput")
    c_mask = dram("c_mask", [128, 16], F32, "ExternalInput")
    uf_p = dram("uf_p", [SP, 512], BF16, DBG)
    us_p = dram("us_p", [SP, 512], BF16, DBG)
    uf_s = dram("uf_s", [SS, 512], BF16, "Internal")
    us_s = dram("us_s", [SS, 512], BF16, "Internal")
    V_p = dram("V_p", [1024, SP], BF16, DBG)
    V_s = dram("V_s", [1024, OWN], BF16, DBG)
    ys_p = dram("ys_p", [SP, 512], BF16, DBG)
    ys_s = dram("ys_s", [OWN, 512], BF16, DBG)
    U_sc = dram("U_sc", [10, 2, 128, 2048], BF16, "Internal")
    XD_p = dram("XD_p", [8, 2, 128, 4096], BF16, "Internal")
    XD_s = dram("XD_s", [2, 2, 128, 4096], BF16, "Internal")
    wcat = dram("wcat", [60, 128, 8, 128], BF16, "Internal")
    wd_sc = dram("wd_sc", [FF, D], BF16, "Internal")

    B_ufp, B_usp, B_ufs, B_uss = Buf("ufp"), Buf("usp"), Buf("ufs"), Buf("uss")
    B_Vp, B_Vs, B_ysp, B_yss = Buf("Vp"), Buf("Vs"), Buf("ysp"), Buf("yss")
    B_Usc, B_XDp, B_XDs, B_wcat, B_wd = Buf("Usc"), Buf("XDp"), Buf("XDs"), Buf("wcat"), Buf("wdsc")
    B_dummy_out = Buf("yout")

    def sb(stack, name, shape, dt):
        t = stack.enter_context(nc.sbuf_tensor(name, list(shape), dt))
        return t, Buf(name)

    psf = []
    for i in range(6):
        t = es.enter_context(nc.psum_tensor("psf%d" % i, [128, 512], F32))
        psf.append((t, Buf("psf%d" % i)))
    psb = []
    for i in range(2):
        t = es.enter_context(nc.psum_tensor("psb%d" % i, [128, 1024], BF16))
        psb.append((t, Buf("psb%d" % i)))
    PF = Ring(psf)
    PB = Ring(psb)

    identb, B_identb = sb(es, "identb", [128, 128], BF16)
    identf, B_identf = sb(es, "identf", [128, 128], F32)
    gt, B_gt = sb(es, "gt", [128, 4, 8], F32)
    epst, B_epst = sb(es, "epst", [128, 1], F32)
    S.op("dve", lambda e: e.memset(epst[:], EPS), writes=[B_epst])

    def rsq(ap, B_ap):
        S.op("act", lambda e: e.activation(out=ap, in_=ap, func=AF.Sqrt, scale=1.0 / D, bias=epst[:, 0:1]),
             reads=[B_ap, B_epst], writes=[B_ap])
        S.op("dve", lambda e: e.reciprocal(out=ap, in_=ap), reads=[B_ap], writes=[B_ap])
    S.dma("sp", B_identb, [(identb[:], c_identb[:, :])], writes=[B_identb])
    S.dma("sp", B_identf, [(identf[:], c_identf[:, :])], writes=[B_identf])
    S.dma("sp", B_gt, [(gt[:, w, :], gains[w].rearrange("(c p) -> p c", p=128)) for w in range(4)],
          writes=[B_gt], allow_slow_non_contiguous=True)

    ph_a = ExitStack()
    wu_res, B_wu = sb(ph_a, "wu_res", [128, 8, 1024], BF16)
    with ExitStack() as st:
        ld = [sb(st, "cv_ld%d" % i, [128, 3072], F32) for i in range(2)]
        cv = [sb(st, "cv_o%d" % i, [128, 2816], BF16) for i in range(2)]
        LD, CV = Ring(ld), Ring(cv)
        for dc in range(8):
            lt, lb = LD.next()
            S.dma("sp", lb, [(lt[:, 0:3072], w_in[dc * 128:(dc + 1) * 128, :])], writes=[lb])
            S.op("act", lambda e, lt=lt, dc=dc: e.activation(out=wu_res[:, dc, :], in_=lt[:, 0:1024], func=AF.Copy,
                                                             scale=gt[:, 0, dc:dc + 1]),
                 reads=[lb, B_gt], writes=[B_wu])
            ct, cb = CV.next()
            S.op("act", lambda e, lt=lt, ct=ct, dc=dc: e.activation(out=ct[:, 0:2048], in_=lt[:, 1024:3072], func=AF.Copy,
                                                                    scale=gt[:, 0, dc:dc + 1]),
                 reads=[lb, B_gt], writes=[cb])
            S.dma("pool", cb, [(wcat[0:16, :, dc, :].rearrange("m p f -> p m f"),
                                ct[:, 0:2048].rearrange("p (m f) -> p m f", f=128))],
                  reads=[cb], writes=[B_wcat])
        for wi, (wsrc, m0) in enumerate(((w_fg, 16), (w_fu, 38))):
            for dc in range(8):
                lt, lb = LD.next()
                S.dma("sp", lb, [(lt[:, 0:FF], wsrc[dc * 128:(dc + 1) * 128, :])], writes=[lb])
                ct, cb = CV.next()
                S.op("act", lambda e, lt=lt, ct=ct, dc=dc: e.activation(out=ct[:, 0:FF], in_=lt[:, 0:FF], func=AF.Copy,
                                                                        scale=gt[:, 2, dc:dc + 1]),
                     reads=[lb, B_gt], writes=[cb])
                S.dma("pool", cb, [(wcat[m0:m0 + NFC, :, dc, :].rearrange("m p f -> p m f"),
                                    ct[:, 0:FF].rearrange("p (m f) -> p m f", f=128))],
                      reads=[cb], writes=[B_wcat])
        for fc in range(NFC):
            lt, lb = LD.next()
            S.dma("sp", lb, [(lt[:, 0:1024], w_fd[fc * 128:(fc + 1) * 128, :])], writes=[lb])
            ct, cb = CV.next()
            S.op("act", lambda e, lt=lt, ct=ct: e.activation(out=ct[:, 0:1024], in_=lt[:, 0:1024], func=AF.Copy),
                 reads=[lb], writes=[cb])
            S.dma("pool", cb, [(wd_sc[fc * 128:(fc + 1) * 128, :], ct[:, 0:1024])], reads=[cb], writes=[B_wd])

    S.barrier()
    if stop == "0":
        S.finish()
        return nc
    def rms_rstd(stack_tiles, src_ap_fn, nk, ss, B_ss, rstd, B_rstd, junk, B_junk, reads):
        for k in range(nk):
            S.op("act", lambda e, k=k: e.activation(out=junk[:], in_=src_ap_fn(k), func=AF.Square,
                                                    accum_out=ss[:, k:k + 1]),
                 reads=reads, writes=[B_junk, B_ss])
        S.op("dve", lambda e: e.tensor_copy(out=rstd[:, 0:nk], in_=ss[:, 0:nk]), reads=[B_ss], writes=[B_rstd])
        rsq(rstd[:, 0:nk], B_rstd)

    def norm_hb(xt, B_xt, rstd, B_rstd, hb, B_hb):
        for k in range(4):
            S.op("act", lambda e, k=k: e.activation(out=hb[:, k, :], in_=xt[:, k, :], func=AF.Copy,
                                                    scale=rstd[:, k:k + 1]),
                 reads=[B_xt, B_rstd], writes=[B_hb])

    def transpose_hT(hb, B_hb, hT, B_hT):
        for dp in range(4):
            pt, pb = PB.next()

            def tr(e, dp=dp, pt=pt):
                ins = None
                for d2 in range(2):
                    for k in range(4):
                        dc = dp * 2 + d2
                        ins = e.transpose(pt[:, (d2 * 4 + k) * 128:(d2 * 4 + k + 1) * 128],
                                          hb[:, k, dc * 128:(dc + 1) * 128], identb[:])
                return ins
            S.op("pe", tr, reads=[B_hb, B_identb], writes=[pb])
            S.op("dve", lambda e, dp=dp, pt=pt: e.tensor_copy(
                out=hT[:, dp * 2:dp * 2 + 2, :], in_=pt[:].rearrange("p (a t) -> p a t", a=2)),
                reads=[pb], writes=[B_hT])

    def norm_transpose(xt, B_xt, rstd, B_rstd, hb, B_hb, hT, B_hT):
        norm_hb(xt, B_xt, rstd, B_rstd, hb, B_hb)
        transpose_hT(hb, B_hb, hT, B_hT)

    with ExitStack() as st:
      if "A" in phases:
          xts = [sb(st, "a_xt%d" % i, [128, 4, 1024], F32) for i in range(2)]
          XT = Ring(xts)
          HB = Ring([sb(st, "a_hb%d" % i, [128, 4, 1024], BF16) for i in range(2)])
          HT = Ring([sb(st, "a_hT%d" % i, [128, 8, 512], BF16) for i in range(2)])
          uos = [sb(st, "a_uo%d" % i, [128, 4, 1024], BF16) for i in range(2)]
          UO = Ring(uos)
          SSR = Ring([sb(st, "a_ss%d" % i, [128, 4], F32) for i in range(2)])
          RSR = Ring([sb(st, "a_rstd%d" % i, [128, 4], F32) for i in range(2)])
          JK = Ring([sb(st, "a_junk%d" % i, [128, 1024], F32) for i in range(2)])
          jobs = [(x_p, uf_p, us_p, B_ufp, B_usp, t) for t in range(SP // 512)] + \
                 [(x_s, uf_s, us_s, B_ufs, B_uss, t) for t in range(SS // 512)]
          if "a" in phases:
              jobs = jobs[:3]

          def a_front_act(job):
              (xsrc, ufd, usd, Bf, Bs, t) = job
              xt, B_xt = XT.next()
              hb, B_hb = HB.next()
              ss, B_ss = SSR.next()
              rstd, B_rstd = RSR.next()
              junk, B_junk = JK.next()
              S.dma("sp", B_xt, [(xt[:], xsrc[t * 512:(t + 1) * 512, :].rearrange("(k p) d -> p k d", p=128))],
                    writes=[B_xt])
              rms_rstd(None, lambda k, xt=xt: xt[:, k, :], 4, ss, B_ss, rstd, B_rstd, junk, B_junk, [B_xt])
              norm_hb(xt, B_xt, rstd, B_rstd, hb, B_hb)
              return (hb, B_hb)

          def a_front_pe(fr):
              hb, B_hb = fr
              hT, B_hT = HT.next()
              transpose_hT(hb, B_hb, hT, B_hT)
              return (hT, B_hT)

          def a_back(job, hTt):
              (xsrc, ufd, usd, Bf, Bs, t) = job
              hT, B_hT = hTt
              uo, B_uo = UO.next()
              for k in range(4):
                  for hf in range(2):
                      pt, pb = PF.next()

                      def mm(e, k=k, hf=hf, pt=pt):
                          ins = None
                          for dc in range(8):
                              ins = e.matmul(pt[:], lhsT=hT[:, dc, k * 128:(k + 1) * 128],
                                             rhs=wu_res[:, dc, hf * 512:(hf + 1) * 512],
                                             start=(dc == 0), stop=(dc == 7))
                          return ins
                      S.op("pe", mm, reads=[B_hT, B_wu], writes=[pb])
                      S.op("act", lambda e, k=k, hf=hf, pt=pt, uo=uo: e.activation(
                          out=uo[:, k, hf * 512:(hf + 1) * 512], in_=pt[:], func=AF.Copy),
                          reads=[pb], writes=[B_uo])
              rows = slice(t * 512, (t + 1) * 512)
              S.dma("pool", B_uo, [(ufd[rows, :].rearrange("(k p) c -> p k c", p=128), uo[:, :, 0:512]),
                                   (usd[rows, :].rearrange("(k p) c -> p k c", p=128), uo[:, :, 512:1024])],
                    reads=[B_uo], writes=[Bf, Bs])

          fr = a_front_act(jobs[0])
          hTt = a_front_pe(fr)
          for ji, job in enumerate(jobs):
              fr = a_front_act(jobs[ji + 1]) if ji + 1 < len(jobs) else None
              a_back(job, hTt)
              if fr is not None:
                  hTt = a_front_pe(fr)
    S.barrier()
    ph_a.close()
    if stop == "A":
        S.finish()
        return nc

    def fft_seq(st, tag, ufd, Bf, Vd, BV, N2, K2, gtab_dram):
        SO = K2 * 128
        W2 = 2 * K2
        nslot = 512 // W2
        f1, B_f1 = sb(st, tag + "f1", [128, 2, 128], BF16)
        gtab, B_gtab = sb(st, tag + "gt", [N2, 128, 3 * K2], BF16)
        S.dma("sp", B_f1, [(f1[:], c_f1[:, :, :])], writes=[B_f1])
        S.dma("sp", B_gtab, [(gtab[:], gtab_dram[:, :, :])], writes=[B_gtab])
        xgs = [sb(st, tag + "xg%d" % i, [128, N2, 128], BF16) for i in range(2)]
        XG = Ring(xgs)
        aps = [sb(st, tag + "ap%d" % i, [N2, 128, 2, 64], BF16) for i in range(2)]
        APR = Ring(aps)
        vts = [sb(st, tag + "vt%d" % i, [128, 2, SO], BF16) for i in range(2 if SO <= 2048 else 1)]
        VT = Ring(vts)
        for g in range(4):
            xg, B_xg = XG.next()
            S.dma("sp", B_xg, [(xg[:], ufd[:, g * 128:(g + 1) * 128].rearrange("(a b) c -> a b c", b=N2))],
                  reads=[Bf], writes=[B_xg])
            vt, B_vt = VT.next()
            for kh in range(2):
                apt, B_ap = APR.next()
                for cq in range(32):
                    pt, pb = PF.next()

                    def s1(e, cq=cq, pt=pt, xg=xg, kh=kh):
                        ins = None
                        for c4 in range(4):
                            ins = e.matmul(pt[0:N2, c4 * 128:(c4 + 1) * 128], lhsT=xg[:, :, cq * 4 + c4],
                                           rhs=f1[:, kh, :], start=True, stop=True)
                        return ins
                    S.op("pe", s1, reads=[B_xg, B_f1], writes=[pb])
                    eng = "act" if (cq % 2 == 0) else "dve"
                    src = pt[0:N2, :]
                    dst = apt[:, cq * 4:cq * 4 + 4, :, :].rearrange("p c r k -> p (c r k)")
                    if eng == "act":
                        S.op("act", lambda e, src=src, dst=dst: e.activation(out=dst, in_=src, func=AF.Copy),
                             reads=[pb], writes=[B_ap])
                    else:
                        S.op("dve", lambda e, src=src, dst=dst: e.tensor_copy(out=dst, in_=src),
                             reads=[pb], writes=[B_ap])
                for kb in range(64 // nslot):
                    pt, pb = PF.next()

                    def s2(e, kb=kb, pt=pt, apt=apt, kh=kh):
                        ins = None
                        for sl in range(nslot):
                            k1h = kb * nslot + sl
                            k1 = kh * 64 + k1h
                            e.matmul(pt[:, sl * W2:(sl + 1) * W2], lhsT=apt[:, :, 0, k1h], rhs=gtab[:, k1, 0:W2],
                                     start=True, stop=False)
                            ins = e.matmul(pt[:, sl * W2:(sl + 1) * W2], lhsT=apt[:, :, 1, k1h],
                                           rhs=gtab[:, k1, K2:3 * K2], start=False, stop=True)
                        return ins
                    S.op("pe", s2, reads=[B_ap, B_gtab], writes=[pb])
                    k1_0 = kh * 64 + kb * nslot
                    src = pt[:].rearrange("p (s r k) -> p s r k", s=nslot, r=2)
                    dst = vt[:].rearrange("p r (k q) -> p q r k", q=128)[:, k1_0:k1_0 + nslot, :, :]
                    if kb % 2 == 0:
                        S.op("act", lambda e, src=src, dst=dst: e.activation(out=dst, in_=src, func=AF.Copy),
                             reads=[pb], writes=[B_vt])
                    else:
                        S.op("dve", lambda e, src=src, dst=dst: e.tensor_copy(out=dst, in_=src),
                             reads=[pb], writes=[B_vt])
            S.dma("pool", B_vt, [(Vd[(g * 2 + r) * 128:(g * 2 + r + 1) * 128, :], vt[:, r, :]) for r in range(2)],
                  reads=[B_vt], writes=[BV])

    if "F" in phases:
        with ExitStack() as st:
            fft_seq(st, "fp_", uf_p, B_ufp, V_p, B_Vp, 64, 64, c_gp)
            S.barrier()
        with ExitStack() as st:
            fft_seq(st, "fs_", uf_s, B_ufs, V_s, B_Vs, 128, 16, c_gs)
            S.barrier()

    ph_b = ExitStack()
    wzt = {}
    for nm in ("rf", "rb", "if", "ib"):
        wzt[nm] = sb(ph_b, "wz_" + nm, [128, 32, 128], BF16)
    woRf, B_woRf = sb(ph_b, "woRf", [128, 32, 128], BF16)
    woRb, B_woRb = sb(ph_b, "woRb", [128, 32, 128], BF16)
    woIf, B_woIf = sb(ph_b, "woIf", [128, 32, 128], BF16)
    woIb, B_woIb = sb(ph_b, "woIb", [128, 32, 128], BF16)
    kloc, B_kloc = sb(ph_b, "kloc", [128, 32, 128], BF16)
    av, B_av = sb(ph_b, "av_p", [128, 32], F32)
    th, B_th = sb(ph_b, "th_p", [128, 32], F32)
    maskt, B_maskt = sb(ph_b, "maskt", [128, 16], F32)
    S.dma("sp", B_maskt, [(maskt[:], c_mask[:, :])], writes=[B_maskt])

    def reduce_pi(st, tag, dst, B_dst, src, B_src, shape, shift, tmps=None):
        if tmps is None:
            tmps = (sb(st, tag + "_ri", shape, mybir.dt.int32), sb(st, tag + "_rf", shape, F32))
        (ti, B_ti), (tf, B_tf) = tmps
        S.op("dve", lambda e: e.tensor_scalar(out=dst, in0=src, scalar1=shift, scalar2=None, op0=ALU.add),
             reads=[B_src], writes=[B_dst])
        S.op("dve", lambda e: e.tensor_scalar(out=tf[:], in0=dst, scalar1=1.0 / TWO_PI, scalar2=None, op0=ALU.mult),
             reads=[B_dst], writes=[B_tf])
        S.op("dve", lambda e: e.tensor_copy(out=ti[:], in_=tf[:]), reads=[B_tf], writes=[B_ti])
        S.op("dve", lambda e: e.tensor_copy(out=tf[:], in_=ti[:]), reads=[B_ti], writes=[B_tf])
        S.op("dve", lambda e: e.scalar_tensor_tensor(out=dst, in0=tf[:], scalar=-TWO_PI, in1=dst, op0=ALU.mult, op1=ALU.add),
             reads=[B_tf, B_dst], writes=[B_dst])
        for (cmp_, thr, add) in ((ALU.is_gt, math.pi, -TWO_PI), (ALU.is_lt, -math.pi, TWO_PI)):
            S.op("dve", lambda e, cmp_=cmp_, thr=thr: e.tensor_scalar(out=tf[:], in0=dst, scalar1=thr, scalar2=None, op0=cmp_),
                 reads=[B_dst], writes=[B_tf])
            S.op("dve", lambda e, add=add: e.scalar_tensor_tensor(out=dst, in0=tf[:], scalar=add, in1=dst, op0=ALU.mult, op1=ALU.add),
                 reads=[B_tf, B_dst], writes=[B_dst])
        S.op("dve", lambda e: e.tensor_scalar(out=dst, in0=dst, scalar1=3.14159, scalar2=-3.14159, op0=ALU.min, op1=ALU.max),
             reads=[B_dst], writes=[B_dst])

    def sincos(st, tag, ang, B_ang, shape, cos_out, B_cos, sin_out, B_sin, off):
        tmp, B_tmp = sb(st, tag + "_sc", shape, F32)
        tmps = (sb(st, tag + "_ri", shape, mybir.dt.int32), sb(st, tag + "_rf", shape, F32))
        reduce_pi(st, tag + "s", tmp[:], B_tmp, ang, B_ang, shape, 0.0, tmps)
        S.op("act", lambda e: e.activation(out=sin_out, in_=tmp[:], func=AF.Sin), reads=[B_tmp], writes=[B_sin])
        reduce_pi(st, tag + "c", tmp[:], B_tmp, ang, B_ang, shape, 0.5 * math.pi, tmps)
        S.op("act", lambda e: e.activation(out=cos_out, in_=tmp[:], func=AF.Sin), reads=[B_tmp], writes=[B_cos])

    negpi, B_negpi = sb(ph_b, "negpi", [128, 1], F32)
    S.op("dve", lambda e: e.memset(negpi[:], -math.pi), writes=[B_negpi])

    with ExitStack() as st:
        def T(name, shape, dt=F32):
            return sb(st, "tg_" + name, shape, dt)

        def dv(fn, reads, writes):
            S.op("dve", fn, reads=reads, writes=writes)

        def tt(out, a, b, op, reads, writes):
            dv(lambda e: e.tensor_tensor(out=out, in0=a, in1=b, op=op), reads, writes)

        lr, B_lr = T("lr", [128, 32])
        li, B_li = T("li", [128, 32])
        ldt, B_ldt = T("ldt", [128, 32])
        btr, B_btr = T("btr", [128, 32, 16])
        bti, B_bti = T("bti", [128, 32, 16])
        craw_r, B_crr = T("crr", [128, 4, 128])
        craw_i, B_cri = T("cri", [128, 4, 128])
        ctr, B_ctr = T("ctr", [128, 32, 16])
        cti, B_cti = T("cti", [128, 32, 16])
        dsk, B_dsk = T("dsk", [128, 32])
        kv17, B_kv17 = T("kv17", [128, 17])
        mf, B_mf = T("mf", [128, 128])
        mb, B_mb = T("mb", [128, 128])
        S.dma("sp", B_lr, [(lr[d * 64:(d + 1) * 64, :], lam_re[d].rearrange("g p -> p g")) for d in range(2)],
              writes=[B_lr], allow_slow_non_contiguous=True)
        S.dma("sp", B_li, [(li[d * 64:(d + 1) * 64, :], lam_im[d].rearrange("g p -> p g")) for d in range(2)],
              writes=[B_li], allow_slow_non_contiguous=True)
        S.dma("sp", B_ldt, [(ldt[d * 64:(d + 1) * 64, :], log_dt[d:d + 1, :].broadcast_to([64, 32])) for d in range(2)],
              writes=[B_ldt])
        S.dma("sp", B_btr, [(btr[d * 64:(d + 1) * 64, :, :], b_re[d].rearrange("g p h -> p g h")) for d in range(2)],
              writes=[B_btr])
        S.dma("sp", B_bti, [(bti[d * 64:(d + 1) * 64, :, :], b_im[d].rearrange("g p h -> p g h")) for d in range(2)],
              writes=[B_bti])
        S.dma("sp", B_crr, [(craw_r[:, q, d * 64:(d + 1) * 64],
                             c_re[d, q * 8:(q + 1) * 8].rearrange("g h p -> (g h) p"))
                            for q in range(4) for d in range(2)], writes=[B_crr])
        S.dma("sp", B_cri, [(craw_i[:, q, d * 64:(d + 1) * 64],
                             c_im[d, q * 8:(q + 1) * 8].rearrange("g h p -> (g h) p"))
                            for q in range(4) for d in range(2)], writes=[B_cri])
        S.dma("sp", B_dsk, [(dsk[i * 16:(i + 1) * 16, :], d_skip.rearrange("(g h) -> h g", h=16)) for i in range(8)],
              writes=[B_dsk], allow_slow_non_contiguous=True)
        S.dma("sp", B_kv17, [(kv17[:], c_kv17[:, :])], writes=[B_kv17])
        S.dma("sp", B_mf, [(mf[:], c_mf[:, :])], writes=[B_mf])
        S.dma("sp", B_mb, [(mb[:], c_mb[:, :])], writes=[B_mb])
        crawb, B_crawb = T("crawb", [128, 4, 128], BF16)
        for (craw, B_craw, ct_, B_ct) in ((craw_r, B_crr, ctr, B_ctr), (craw_i, B_cri, cti, B_cti)):
            dv(lambda e, craw=craw: e.tensor_copy(out=crawb[:], in_=craw[:]), [B_craw], [B_crawb])
            pt, pb = PB.next()

            def trc(e, pt=pt):
                ins = None
                for q in range(4):
                    ins = e.transpose(pt[:, q * 128:(q + 1) * 128], crawb[:, q, :], identb[:])
                return ins
            S.op("pe", trc, reads=[B_crawb, B_identb], writes=[pb])
            S.op("dve", lambda e, pt=pt, ct_=ct_: e.tensor_copy(out=ct_[:].rearrange("p g h -> p (g h)"), in_=pt[:, 0:512]),
                 reads=[pb], writes=[B_ct])
        dtt, B_dtt = T("dtt", [128, 32])
        S.op("act", lambda e: e.activation(out=dtt[:], in_=ldt[:], func=AF.Exp), reads=[B_ldt], writes=[B_dtt])
        tt(av[:], lr[:], dtt[:], ALU.mult, [B_lr, B_dtt], [B_av])
        tt(th[:], li[:], dtt[:], ALU.mult, [B_li, B_dtt], [B_th])
        if stop == "T1":
            S.finish()
            return nc
        ak, B_ak = T("ak", [128, 32, 17])
        tk, B_tk = T("tk", [128, 32, 17])
        mag, B_mag = T("mag", [128, 32, 17])
        pwr, B_pwr = T("pwr", [128, 32, 17])
        pwi, B_pwi = T("pwi", [128, 32, 17])
        kvb = kv17[:].unsqueeze(1).broadcast_to([128, 32, 17])
        tt(ak[:], av[:].unsqueeze(2).broadcast_to([128, 32, 17]), kvb, ALU.mult, [B_av, B_kv17], [B_ak])
        tt(tk[:], th[:].unsqueeze(2).broadcast_to([128, 32, 17]), kvb, ALU.mult, [B_th, B_kv17], [B_tk])
        S.op("act", lambda e: e.activation(out=mag[:], in_=ak[:], func=AF.Exp), reads=[B_ak], writes=[B_mag])
        sincos(st, "pw", tk[:], B_tk, [128, 32, 17], pwr[:], B_pwr, pwi[:], B_pwi, 64 * math.pi)
        tt(pwr[:], pwr[:], mag[:], ALU.mult, [B_pwr, B_mag], [B_pwr])
        tt(pwi[:], pwi[:], mag[:], ALU.mult, [B_pwi, B_mag], [B_pwi])
        if stop == "T2":
            S.finish()
            return nc
        n2, B_n2 = T("n2", [128, 32])
        t1, B_t1 = T("t1", [128, 32])
        t2, B_t2 = T("t2", [128, 32])
        qr, B_qr = T("qr", [128, 32])
        qi, B_qi = T("qi", [128, 32])
        l1r, B_l1r = T("l1r", [128, 32])
        tt(n2[:], lr[:], lr[:], ALU.mult, [B_lr], [B_n2])
        tt(t1[:], li[:], li[:], ALU.mult, [B_li], [B_t1])
        tt(n2[:], n2[:], t1[:], ALU.add, [B_n2, B_t1], [B_n2])
        dv(lambda e: e.reciprocal(out=n2[:], in_=n2[:]), [B_n2], [B_n2])
        dv(lambda e: e.tensor_scalar(out=l1r[:], in0=pwr[:, :, 9], scalar1=-1.0, scalar2=None, op0=ALU.add),
           [B_pwr], [B_l1r])
        tt(t1[:], l1r[:], lr[:], ALU.mult, [B_l1r, B_lr], [B_t1])
        tt(t2[:], pwi[:, :, 9], li[:], ALU.mult, [B_pwi, B_li], [B_t2])
        tt(qr[:], t1[:], t2[:], ALU.add, [B_t1, B_t2], [B_qr])
        tt(qr[:], qr[:], n2[:], ALU.mult, [B_qr, B_n2], [B_qr])
        tt(t1[:], pwi[:, :, 9], lr[:], ALU.mult, [B_pwi, B_lr], [B_t1])
        tt(t2[:], l1r[:], li[:], ALU.mult, [B_l1r, B_li], [B_t2])
        tt(qi[:], t1[:], t2[:], ALU.subtract, [B_t1, B_t2], [B_qi])
        tt(qi[:], qi[:], n2[:], ALU.mult, [B_qi, B_n2], [B_qi])
        bbr, B_bbr = T("bbr", [128, 32, 16])
        bbi, B_bbi = T("bbi", [128, 32, 16])
        t3, B_t3 = T("t3", [128, 32, 16])
        qrb = qr[:].unsqueeze(2).broadcast_to([128, 32, 16])
        qib = qi[:].unsqueeze(2).broadcast_to([128, 32, 16])
        tt(bbr[:], btr[:], qrb, ALU.mult, [B_btr, B_qr], [B_bbr])
        tt(t3[:], bti[:], qib, ALU.mult, [B_bti, B_qi], [B_t3])
        tt(bbr[:], bbr[:], t3[:], ALU.subtract, [B_bbr, B_t3], [B_bbr])
        tt(bbi[:], bti[:], qrb, ALU.mult, [B_bti, B_qr], [B_bbi])
        tt(t3[:], btr[:], qib, ALU.mult, [B_btr, B_qi], [B_t3])
        tt(bbi[:], bbi[:], t3[:], ALU.add, [B_bbi, B_t3], [B_bbi])

        if stop == "T3":
            S.finish()
            return nc
        big1, B_big1 = T("big1", [128, 32, 8, 16])

        def pw_slices(kind):
            if kind == "wo":
                return (slice(9, 17), slice(16, 8, -1))
            if kind == "bn":
                return (slice(7, None, -1), slice(0, 8))
            if kind == "bz":
                return (slice(15, 7, -1), slice(8, 16))
            raise ValueError(kind)

        def cprod(outR, B_outR, outI, B_outI, kind, mr, B_mr, mi, B_mi, neg_im):
            sl = pw_slices(kind)
            for half in range(2):
                ps_ = slice(half * 64, (half + 1) * 64)
                pr = pwr[ps_, :, sl[half]].unsqueeze(3).broadcast_to([64, 32, 8, 16])
                pi_ = pwi[ps_, :, sl[half]].unsqueeze(3).broadcast_to([64, 32, 8, 16])
                mrb = mr[ps_].unsqueeze(2).broadcast_to([64, 32, 8, 16])
                mib = mi[ps_].unsqueeze(2).broadcast_to([64, 32, 8, 16])
                oR = outR[ps_].rearrange("p g (x y) -> p g x y", y=16)
                oI = outI[ps_].rearrange("p g (x y) -> p g x y", y=16)
                b1 = big1[ps_]
                tt(oR, pr, mrb, ALU.mult, [B_pwr, B_mr], [B_outR])
                tt(b1, pi_, mib, ALU.mult, [B_pwi, B_mi], [B_big1])
                tt(oR, oR, b1, ALU.subtract, [B_outR, B_big1], [B_outR])
                tt(oI, pr, mib, ALU.mult, [B_pwr, B_mi], [B_outI])
                tt(b1, pi_, mrb, ALU.mult, [B_pwi, B_mr], [B_big1])
                if neg_im:
                    tt(oI, oI, b1, ALU.add, [B_outI, B_big1], [B_outI])
                    dv(lambda e, oI=oI: e.tensor_scalar(out=oI, in0=oI, scalar1=-1.0, scalar2=None, op0=ALU.mult),
                       [B_outI], [B_outI])
                else:
                    tt(oI, oI, b1, ALU.add, [B_outI, B_big1], [B_outI])

        woR32, B_woR32 = T("woR32", [128, 32, 128])
        woI32, B_woI32 = T("woI32", [128, 32, 128])
        bnR, B_bnR = T("bnR", [128, 32, 128])
        bnI, B_bnI = T("bnI", [128, 32, 128])
        cprod(woR32, B_woR32, woI32, B_woI32, "wo", ctr, B_ctr, cti, B_cti, True)
        for (tl, B_tl) in ((woRf, B_woRf), (woRb, B_woRb), (woIf, B_woIf), (woIb, B_woIb)):
            S.op("pool", lambda e, tl=tl: e.memset(tl[:], 0.0), writes=[B_tl])
        dv(lambda e: e.tensor_copy(out=woRf[0:64], in_=woR32[0:64]), [B_woR32, B_woRf], [B_woRf])
        dv(lambda e: e.tensor_copy(out=woRb[64:128], in_=woR32[64:128]), [B_woR32, B_woRb], [B_woRb])
        dv(lambda e: e.tensor_copy(out=woIf[0:64], in_=woI32[0:64]), [B_woI32, B_woIf], [B_woIf])
        dv(lambda e: e.tensor_copy(out=woIb[64:128], in_=woI32[64:128]), [B_woI32, B_woIb], [B_woIb])
        cprod(bnR, B_bnR, bnI, B_bnI, "bn", bbr, B_bbr, bbi, B_bbi, False)
        if stop == "T4":
            S.finish()
            return nc
        class _V:
            def __init__(self, ap):
                self.ap = ap

            def __getitem__(self, k):
                return self.ap[k]
        bnRb = _V(woR32[:].rearrange("p g c -> p (g c)").bitcast(BF16)[:, 0:4096].rearrange("p (g c) -> p g c", g=32))
        bnIb = _V(woI32[:].rearrange("p g c -> p (g c)").bitcast(BF16)[:, 0:4096].rearrange("p (g c) -> p g c", g=32))
        B_bnRb, B_bnIb = B_woR32, B_woI32
        dv(lambda e: e.tensor_copy(out=bnRb[:], in_=bnR[:]), [B_bnR], [B_bnRb])
        dv(lambda e: e.tensor_copy(out=bnIb[:], in_=bnI[:]), [B_bnI], [B_bnIb])
        ktmp, B_ktmp = T("ktmp", [128, 128])
        ktmp2, B_ktmp2 = T("ktmp2", [128, 128])
        for g in range(32):
            ptf, pbf = PF.next()

            def kmm(e, g=g, pt=ptf):
                for half, (wr_, wi_) in enumerate(((woRf, woIf), (woRb, woIb))):
                    e.matmul(pt[:, half * 128:(half + 1) * 128], lhsT=bnRb[:, g, :], rhs=wr_[:, g, :],
                             start=True, stop=False)
                    ins = e.matmul(pt[:, half * 128:(half + 1) * 128], lhsT=bnIb[:, g, :], rhs=wi_[:, g, :],
                                   start=False, stop=True)
                return ins
            S.op("pe", kmm, reads=[B_bnRb, B_bnIb, B_woRf, B_woRb, B_woIf, B_woIb], writes=[pbf])
            tt(ktmp[:], ptf[:, 0:128], mf[:], ALU.mult, [pbf, B_mf], [B_ktmp])
            tt(ktmp2[:], ptf[:, 128:256], mb[:], ALU.mult, [pbf, B_mb], [B_ktmp2])
            tt(ktmp[:], ktmp[:], ktmp2[:], ALU.add, [B_ktmp, B_ktmp2], [B_ktmp])
            dv(lambda e, g=g: e.scalar_tensor_tensor(out=kloc[:, g, :], in0=identf[:], scalar=dsk[:, g:g + 1],
                                                     in1=ktmp[:], op0=ALU.mult, op1=ALU.add),
               [B_identf, B_dsk, B_ktmp], [B_kloc])
        if stop == "T5":
            S.finish()
            return nc
        cprod(bnR, B_bnR, bnI, B_bnI, "bz", bbr, B_bbr, bbi, B_bbi, False)
        for nm in ("rf", "rb", "if", "ib"):
            S.op("pool", lambda e, nm=nm: e.memset(wzt[nm][0][:], 0.0), writes=[wzt[nm][1]])
        dv(lambda e: e.tensor_copy(out=bnRb[:], in_=bnR[:]), [B_bnR], [B_bnRb])
        dv(lambda e: e.tensor_copy(out=bnIb[:], in_=bnI[:]), [B_bnI], [B_bnIb])
        for (src, B_src, kf, kb_) in ((bnRb, B_bnRb, "rf", "rb"), (bnIb, B_bnIb, "if", "ib")):
            for gq in range(4):
                pt, pb = PB.next()

                def trz(e, src=src, gq=gq, pt=pt):
                    ins = None
                    for g8 in range(8):
                        ins = e.transpose(pt[:, g8 * 128:(g8 + 1) * 128], src[:, gq * 8 + g8, :], identb[:])
                    return ins
                S.op("pe", trz, reads=[B_src, B_identb], writes=[pb])
                v = pt[:].rearrange("p (g c) -> p g c", g=8)
                dv(lambda e, v=v, kf=kf, gq=gq: e.tensor_copy(out=wzt[kf][0][:, gq * 8:gq * 8 + 8, 0:64], in_=v[:, :, 0:64]),
                   [pb], [wzt[kf][1]])
                dv(lambda e, v=v, kb_=kb_, gq=gq: e.tensor_copy(out=wzt[kb_][0][:, gq * 8:gq * 8 + 8, 64:128],
                                                               in_=v[:, :, 64:128]),
                   [pb], [wzt[kb_][1]])
    S.barrier()
    cosT, B_cosT = sb(ph_b, "cosT", [128, 32, 129], F32)
    sinT, B_sinT = sb(ph_b, "sinT", [128, 32, 129], F32)
    dec, B_dec = sb(ph_b, "dec", [128, 32, 128], F32)
    rho, B_rho = sb(ph_b, "rho", [128, 32], F32)
    with ExitStack() as st:
        def dv(fn, reads, writes):
            S.op("dve", fn, reads=reads, writes=writes)
        kv129, B_kv129 = sb(st, "tg_kv129", [128, 129], F32)
        S.dma("sp", B_kv129, [(kv129[:], c_kv129[:, :])], writes=[B_kv129])
        phr, B_phr = sb(st, "tg_phr", [128, 32], F32)
        angk, B_angk = sb(st, "tg_angk", [128, 32, 129], F32)
        ph8, B_ph8 = sb(st, "tg_ph8", [128, 32], F32)
        dv(lambda e: e.tensor_scalar(out=ph8[:], in0=th[:], scalar1=8.0, scalar2=None, op0=ALU.mult), [B_th], [B_ph8])
        reduce_pi(st, "ph8", phr[:], B_phr, ph8[:], B_ph8, [128, 32], 0.0)
        dv(lambda e: e.tensor_tensor(out=angk[:], in0=phr[:].unsqueeze(2).broadcast_to([128, 32, 129]),
                                     in1=kv129[:].unsqueeze(1).broadcast_to([128, 32, 129]), op=ALU.mult),
           [B_phr, B_kv129], [B_angk])
        sincos(st, "ph", angk[:], B_angk, [128, 32, 129], cosT[:], B_cosT, sinT[:], B_sinT, 0.0)
        S.op("act", lambda e: e.activation(out=rho[:], in_=av[:], func=AF.Exp, scale=8.0), reads=[B_av], writes=[B_rho])
        dv(lambda e: e.tensor_copy(out=dec[:], in_=rho[:].unsqueeze(2).broadcast_to([128, 32, 128])), [B_rho], [B_dec])
        dv(lambda e: e.memset(dec[:, :, 0:1], 0.0), [B_dec], [B_dec])

    S.barrier()
    if stop == "T":
        S.finish()
        return nc
    def ssm_pass1(tag, usd, Bus, nsb, f_order, b_order, own, XDd, BXD, Ubase, use_mask):
        with ExitStack() as st:
            tfs = [sb(st, tag + "T%d" % i, [128, 8, 512], BF16) for i in range(2)]
            tgs = [sb(st, tag + "G%d" % i, [128, 32, 128], BF16) for i in range(2)]
            uf_, B_uf_ = sb(st, tag + "Uf", [128, 16, 128], BF16)
            ub_, B_ub_ = sb(st, tag + "Ub", [128, 16, 128], BF16)
            ztr, B_ztr = sb(st, tag + "ztr", [128, 16, 128], F32)
            zti, B_zti = sb(st, tag + "zti", [128, 16, 128], F32)
            sr, B_sr = sb(st, tag + "sr", [128, 16, 128], F32)
            si, B_si = sb(st, tag + "si", [128, 16, 128], F32)
            dd, B_dd = sb(st, tag + "dd", [128, 2, 16, 128], BF16)
            car, B_car = sb(st, tag + "car", [128, 2, 2, 16], F32)
            c1, B_c1 = sb(st, tag + "c1", [128, 16], F32)
            c2, B_c2 = sb(st, tag + "c2", [128, 16], F32)
            c3, B_c3 = sb(st, tag + "c3", [128, 16], F32)
            S.op("dve", lambda e: e.memset(car[:], 0.0), writes=[B_car])
            for j in range(nsb):
                sf, sbk = f_order[j], b_order[j]
                is_own = own(j)
                tf, B_tf = tfs[0]
                tb, B_tb = tfs[1]
                S.dma("sp", B_tf, [(tf[:], usd[sf * 1024:(sf + 1) * 1024, :].rearrange("(c i) h -> c i h", i=8))],
                      reads=[Bus], writes=[B_tf])
                S.dma("sp", B_tb, [(tb[:], usd[sbk * 1024:(sbk + 1) * 1024, :].rearrange("(c i) h -> c i h", i=8))],
                      reads=[Bus], writes=[B_tb])
                for (tsrc, B_tsrc, (tdst, B_tdst)) in ((tf, B_tf, tgs[0]), (tb, B_tb, tgs[1])):
                    S.op("act", lambda e, tsrc=tsrc, tdst=tdst: e.activation(
                        out=tdst[:].rearrange("p g (i h) -> p i g h", i=8),
                        in_=tsrc[:].rearrange("p i (g h) -> p i g h", g=32), func=AF.Copy), reads=[B_tsrc], writes=[B_tdst])
                for gh in range(2):
                    g0 = gh * 16
                    for (tsrc, B_tsrc, ud, B_ud) in ((tgs[0][0], tgs[0][1], uf_, B_uf_), (tgs[1][0], tgs[1][1], ub_, B_ub_)):
                        for gq in range(2):
                            pt, pb = PB.next()

                            def tru(e, tsrc=tsrc, gq=gq, pt=pt, g0=g0):
                                ins = None
                                for g8 in range(8):
                                    g = g0 + gq * 8 + g8
                                    ins = e.transpose(pt[:, g8 * 128:(g8 + 1) * 128], tsrc[:, g, :], identb[:])
                                return ins
                            S.op("pe", tru, reads=[B_tsrc, B_identb], writes=[pb])
                            S.op("act", lambda e, pt=pt, ud=ud, gq=gq: e.activation(
                                out=ud[:, gq * 8:(gq + 1) * 8, :], in_=pt[:].rearrange("p (g c) -> p g c", g=8),
                                func=AF.Copy), reads=[pb], writes=[B_ud])
                    if is_own:
                        S.dma("pool", B_uf_, [(U_sc[Ubase + sf, gh], uf_[:].rearrange("p g c -> p (g c)"))],
                              reads=[B_uf_], writes=[B_Usc])
                    for gq in range(4):
                        ptr_, pbr = PF.next()
                        pti_, pbi = PF.next()

                        def zmm(e, gq=gq, ptr_=ptr_, pti_=pti_, g0=g0):
                            ins = None
                            for g4 in range(4):
                                gl = gq * 4 + g4
                                g = g0 + gl
                                for (pt, kf, kb_) in ((ptr_, "rf", "rb"), (pti_, "if", "ib")):
                                    e.matmul(pt[:, g4 * 128:(g4 + 1) * 128], lhsT=wzt[kf][0][:, g, :], rhs=uf_[:, gl, :],
                                             start=True, stop=False)
                                    ins = e.matmul(pt[:, g4 * 128:(g4 + 1) * 128], lhsT=wzt[kb_][0][:, g, :],
                                                   rhs=ub_[:, gl, ::-1], start=False, stop=True)
                            return ins
                        S.op("pe", zmm, reads=[B_uf_, B_ub_] + [wzt[k][1] for k in wzt], writes=[pbr, pbi])
                        gs = slice(gq * 4, gq * 4 + 4)
                        ga = slice(g0 + gq * 4, g0 + gq * 4 + 4)
                        zr = ptr_[:].rearrange("p (g c) -> p g c", g=4)
                        zi = pti_[:].rearrange("p (g c) -> p g c", g=4)
                        cs_ = cosT[:, ga, 1:129]
                        sn_ = sinT[:, ga, 1:129]

                        def t2(out, a, b, op, reads, writes):
                            S.op("dve", lambda e: e.tensor_tensor(out=out, in0=a, in1=b, op=op), reads=reads, writes=writes)
                        t2(ztr[:, gs, :], zr, cs_, ALU.mult, [pbr, B_cosT], [B_ztr])
                        t2(sr[:, gs, :], zi, sn_, ALU.mult, [pbi, B_sinT], [B_sr])
                        t2(zti[:, gs, :], zi, cs_, ALU.mult, [pbi, B_cosT], [B_zti])
                        t2(si[:, gs, :], zr, sn_, ALU.mult, [pbr, B_sinT], [B_si])
                    S.op("dve", lambda e: e.tensor_tensor(out=ztr[:], in0=ztr[:], in1=sr[:], op=ALU.add),
                         reads=[B_ztr, B_sr], writes=[B_ztr])
                    S.op("dve", lambda e: e.tensor_tensor(out=zti[:], in0=zti[:], in1=si[:], op=ALU.subtract),
                         reads=[B_zti, B_si], writes=[B_zti])
                    rg = rho[:, g0:g0 + 16]
                    if use_mask:
                        S.op("dve", lambda e, gh=gh, j=j: e.tensor_scalar(
                            out=car[:, gh].rearrange("p r g -> p (r g)"), in0=car[:, gh].rearrange("p r g -> p (r g)"),
                            scalar1=maskt[:, j:j + 1], scalar2=None, op0=ALU.mult),
                            reads=[B_car, B_maskt], writes=[B_car])
                    for (ri, zt_, B_zt) in ((0, ztr, B_ztr), (1, zti, B_zti)):
                        S.op("dve", lambda e, ri=ri, gh=gh: e.tensor_tensor(out=c1[:], in0=car[:, gh, ri, :], in1=rg, op=ALU.mult),
                             reads=[B_car, B_rho], writes=[B_c1])
                        S.op("dve", lambda e, zt_=zt_: e.tensor_tensor(out=zt_[:, :, 0], in0=zt_[:, :, 0], in1=c1[:], op=ALU.add),
                             reads=[B_c1, B_zt], writes=[B_zt])
                    if is_own:
                        for ri in range(2):
                            S.op("dve", lambda e, ri=ri, gh=gh: e.tensor_copy(out=dd[:, ri, :, 0], in_=car[:, gh, ri, :]),
                                 reads=[B_car], writes=[B_dd])
                    decv = dec[:, g0:g0 + 16, :].rearrange("p g c -> p (g c)")
                    for (zt_, B_zt, s_, B_s) in ((ztr, B_ztr, sr, B_sr), (zti, B_zti, si, B_si)):
                        S.op("dve", lambda e, zt_=zt_, s_=s_: e.tensor_tensor_scan(
                            out=s_[:].rearrange("p g c -> p (g c)"), data0=decv,
                            data1=zt_[:].rearrange("p g c -> p (g c)"), initial=0.0, op0=ALU.mult, op1=ALU.add),
                            reads=[B_zt, B_dec], writes=[B_s])
                    cl = cosT[:, g0:g0 + 16, 128]
                    sl_ = sinT[:, g0:g0 + 16, 128]

                    def t3(out, a, b, op, reads, writes):
                        S.op("dve", lambda e: e.tensor_tensor(out=out, in0=a, in1=b, op=op), reads=reads, writes=writes)
                    t3(c1[:], sr[:, :, 127], cl, ALU.mult, [B_sr, B_cosT], [B_c1])
                    t3(c2[:], si[:, :, 127], sl_, ALU.mult, [B_si, B_sinT], [B_c2])
                    t3(c3[:], sr[:, :, 127], sl_, ALU.mult, [B_sr, B_sinT], [B_c3])
                    t3(car[:, gh, 0, :], c1[:], c2[:], ALU.subtract, [B_c1, B_c2], [B_car])
                    t3(c1[:], si[:, :, 127], cl, ALU.mult, [B_si, B_cosT], [B_c1])
                    t3(car[:, gh, 1, :], c3[:], c1[:], ALU.add, [B_c3, B_c1], [B_car])
                    if is_own:
                        ga = slice(g0, g0 + 16)
                        cs_ = cosT[:, ga, 1:128]
                        sn_ = sinT[:, ga, 1:128]
                        t3(ztr[:, :, 0:127], sr[:, :, 0:127], cs_, ALU.mult, [B_sr, B_cosT], [B_ztr])
                        S.op("pool", lambda e: e.tensor_tensor(out=zti[:, :, 0:127], in0=si[:, :, 0:127], in1=sn_, op=ALU.mult),
                             reads=[B_si, B_sinT], writes=[B_zti])
                        t3(dd[:, 0, :, 1:128], ztr[:, :, 0:127], zti[:, :, 0:127], ALU.subtract, [B_ztr, B_zti], [B_dd])
                        t3(ztr[:, :, 0:127], sr[:, :, 0:127], sn_, ALU.mult, [B_sr, B_sinT, B_dd], [B_ztr])
                        S.op("pool", lambda e: e.tensor_tensor(out=zti[:, :, 0:127], in0=si[:, :, 0:127], in1=cs_, op=ALU.mult),
                             reads=[B_si, B_cosT, B_dd], writes=[B_zti])
                        t3(dd[:, 1, :, 1:128], ztr[:, :, 0:127], zti[:, :, 0:127], ALU.add, [B_ztr, B_zti], [B_dd])
                        S.dma("pool", B_dd, [(XDd[own(j, True), gh], dd[:].rearrange("p r g c -> p (r g c)"))],
                              reads=[B_dd], writes=[BXD])

    def ssm_pass2(tag, nown, XDd, BXD, slot_f, slot_b, Ubase, ysd, Bys):
        with ExitStack() as st:
            ut, B_ut = sb(st, tag + "u", [128, 16, 128], BF16)
            xs_, B_xs = sb(st, tag + "x", [128, 2, 16, 128], BF16)
            yg, B_yg = sb(st, tag + "yg", [128, 32, 128], BF16)
            tts = [sb(st, tag + "tt%d" % i, [128, 8, 512], BF16) for i in range(2)]
            TT = Ring(tts)
            for s in range(nown):
                for gh in range(2):
                    S.dma("sp", B_ut, [(ut[:].rearrange("p g c -> p (g c)"), U_sc[Ubase + s, gh])],
                          reads=[B_Usc], writes=[B_ut])
                    S.dma("sp", B_xs, [(xs_[0:64].rearrange("p r g c -> p (r g c)"), XDd[slot_f(s), gh, 0:64, :]),
                                       (xs_[64:128].rearrange("p r g c -> p (r g c)"), XDd[slot_b(s), gh, 64:128, :])],
                          reads=[BXD], writes=[B_xs])
                    for gq in range(4):
                        pt, pb = PF.next()

                        def ymm(e, gq=gq, pt=pt, gh=gh):
                            ins = None
                            for g4 in range(4):
                                gl = gq * 4 + g4
                                g = gh * 16 + gl
                                o = pt[:, g4 * 128:(g4 + 1) * 128]
                                e.matmul(o, lhsT=kloc[:, g, :], rhs=ut[:, gl, :], start=True, stop=False)
                                e.matmul(o, lhsT=woRf[:, g, :], rhs=xs_[:, 0, gl, :], start=False, stop=False)
                                e.matmul(o, lhsT=woIf[:, g, :], rhs=xs_[:, 1, gl, :], start=False, stop=False)
                                e.matmul(o, lhsT=woRb[:, g, :], rhs=xs_[:, 0, gl, ::-1], start=False, stop=False)
                                ins = e.matmul(o, lhsT=woIb[:, g, :], rhs=xs_[:, 1, gl, ::-1], start=False, stop=True)
                            return ins
                        S.op("pe", ymm, reads=[B_ut, B_xs, B_kloc, B_woRf, B_woRb, B_woIf, B_woIb], writes=[pb])
                        S.op("act", lambda e, pt=pt, gq=gq, gh=gh: e.activation(
                            out=yg[:, gh * 16 + gq * 4:gh * 16 + gq * 4 + 4, :],
                            in_=pt[:].rearrange("p (g c) -> p g c", g=4), func=AF.Copy), reads=[pb], writes=[B_yg])
                tt_, B_tt = TT.next()
                for gq in range(4):
                    pt, pb = PB.next()

                    def try_(e, gq=gq, pt=pt):
                        ins = None
                        for g8 in range(8):
                            ins = e.transpose(pt[:, g8 * 128:(g8 + 1) * 128], yg[:, gq * 8 + g8, :], identb[:])
                        return ins
                    S.op("pe", try_, reads=[B_yg, B_identb], writes=[pb])
                    src = pt[:].rearrange("p (g j h) -> p g j h", g=8, j=8)
                    dst = tt_[:, :, gq * 128:(gq + 1) * 128].rearrange("p j (g h) -> p g j h", g=8)
                    S.op("dve", lambda e, src=src, dst=dst: e.tensor_copy(out=dst, in_=src), reads=[pb], writes=[B_tt])
                S.dma("pool", B_tt, [(ysd[s * 1024:(s + 1) * 1024, :].rearrange("(c j) h -> c j h", j=8), tt_[:])],
                      reads=[B_tt], writes=[Bys])

    def own_p(j, slot=False):
        return j if slot else True

    def own_s(j, slot=False):
        return (j - 14) if slot else (j >= 14)

    if "S" in phases:
      ssm_pass1("s1p_", us_p, B_usp, 8, list(range(8)), list(range(7, -1, -1)), own_p, XD_p, B_XDp, 0, False)
      S.barrier()
      ssm_pass2("s2p_", 8, XD_p, B_XDp, lambda s: s, lambda s: 7 - s, 0, ys_p, B_ysp)
      S.barrier()
      ssm_pass1("s1s_", us_s, B_uss, 16, [(2 + j) % 16 for j in range(16)], list(range(15, -1, -1)), own_s, XD_s, B_XDs, 8, True)
      S.barrier()
      ssm_pass2("s2s_", 2, XD_s, B_XDs, lambda s: s, lambda s: 1 - s, 8, ys_s, B_yss)
      S.barrier()
    ph_b.close()
    if stop == "S":
        S.finish()
        return nc

    ph_c = ExitStack()
    g2t, B_g2t = sb(ph_c, "g2t", [128, 1024], F32)
    g4t, B_g4t = sb(ph_c, "g4t", [128, 1024], F32)
    S.dma("sp", B_g2t, [(g2t[:], gains[1:2, :].broadcast_to([128, 1024]))], writes=[B_g2t])
    S.dma("sp", B_g4t, [(g4t[:], gains[3:4, :].broadcast_to([128, 1024]))], writes=[B_g4t])
    wfp, B_wfp = sb(ph_c, "wfp", [128, 8, 1024], BF16)
    wv, B_wv = sb(ph_c, "wv", [128, 4, 1024], BF16)
    wgg, B_wgg = sb(ph_c, "wgg", [128, 4, 1024], BF16)
    wo_, B_wo = sb(ph_c, "wo", [128, 8, 1024], BF16)
    with ExitStack() as st:
        ld = [sb(st, "cw_ld%d" % i, [128, 1024], F32) for i in range(3)]
        LD = Ring(ld)
        ccs32, B_ccs32 = sb(st, "ccs32", [128, 256], F32)
        ccs, B_ccs = sb(st, "ccs", [128, 256], BF16)
        ltb, B_ltb = sb(st, "cw_ltb", [128, 1024], BF16)
        S.dma("sp", B_ccs32, [(ccs32[:], c_cs[:, :])], writes=[B_ccs32])
        S.op("dve", lambda e: e.tensor_copy(out=ccs[:], in_=ccs32[:]), reads=[B_ccs32], writes=[B_ccs])
        for (wsrc, nkc, dst, B_dst) in ((w_gv, 4, wv, B_wv), (w_gg, 4, wgg, B_wgg), (w_o, 8, wo_, B_wo)):
            for kc in range(nkc):
                lt, lb = LD.next()
                S.dma("sp", lb, [(lt[:], wsrc[kc * 128:(kc + 1) * 128, :])], writes=[lb])
                S.op("act", lambda e, lt=lt, dst=dst, kc=kc: e.activation(out=dst[:, kc, :], in_=lt[:], func=AF.Copy),
                     reads=[lb], writes=[B_dst])
        for g in range(4):
            lt, lb = LD.next()
            S.dma("sp", lb, [(lt[:], w_fn[g * 128:(g + 1) * 128, :])], writes=[lb])
            S.op("dve", lambda e, lt=lt: e.tensor_copy(out=ltb[:], in_=lt[:]), reads=[lb], writes=[B_ltb])
            for r in range(2):
                for hf in range(2):
                    pt, pb = PF.next()
                    S.op("pe", lambda e, pt=pt, lt=lt, r=r, hf=hf: e.matmul(
                        pt[:], lhsT=ccs[:, r * 128:(r + 1) * 128], rhs=ltb[:, hf * 512:(hf + 1) * 512], start=True, stop=True),
                        reads=[B_ltb, B_ccs], writes=[pb])
                    S.op("act", lambda e, pt=pt, g=g, r=r, hf=hf: e.activation(
                        out=wfp[:, g * 2 + r, hf * 512:(hf + 1) * 512], in_=pt[:], func=AF.Copy),
                        reads=[pb], writes=[B_wfp])

    S.barrier()
    if stop == "R":
        S.finish()
        return nc
    with ExitStack() as st:
        xt, B_xt = sb(st, "c_xt", [128, 4, 1024], F32)
        XK = Ring([sb(st, "c_xk%d" % i, [128, 1024], F32) for i in range(2)])
        hbF, B_hbF = sb(st, "c_hb", [128, 4, 1024], BF16)
        hbH, B_hbH = hbF, B_hbF
        hT1, B_hT1 = sb(st, "c_hT1", [128, 8, 512], BF16)
        hT3, B_hT3 = sb(st, "c_hT3", [128, 8, 512], BF16)
        sg, B_sg = sb(st, "c_sg", [128, 16, 512], BF16)
        vt, B_vt = sb(st, "c_vt", [128, 8, 512], BF16)
        mg, B_mg = sb(st, "c_mg", [128, 8, 512], BF16)
        yt, B_yt = sb(st, "c_yt", [128, 4, 512], BF16)
        zT, B_zT = sb(st, "c_zT", [128, 4, 512], BF16)
        zp, B_zp = sb(st, "c_zp", [128, 512], F32)
        zq, B_zq = sb(st, "c_zq", [128, 512], F32)
        af, B_af = sb(st, "c_af", [128, NFC, 512], BF16)
        SGT = Ring([sb(st, "c_sgt%d" % i, [128, 512], BF16) for i in range(2)])
        TM = Ring([sb(st, "c_tm%d" % i, [128, 512], F32) for i in range(2)])
        WST = Ring([sb(st, "c_ws%d" % i, [128, 8, 128], BF16) for i in range(6)])
        WDS = Ring([sb(st, "c_wd%d" % i, [128, 1024], BF16) for i in range(4)])
        ss, B_ss = sb(st, "c_ss", [128, 8], F32)
        ssF, B_ssF = sb(st, "c_ssF", [128, 4], F32)
        rstd, B_rstd = sb(st, "c_rstd", [128, 4], F32)
        rstdF, B_rstdF = sb(st, "c_rstdF", [128, 4], F32)
        junk, B_junk = sb(st, "c_junk", [128, 1024], BF16)
        OB = Ring([sb(st, "c_ob%d" % i, [128, 1024], F32) for i in range(2)])

        def load_w(m):
            wt, wb = WST.next()
            S.dma("sp", wb, [(wt[:].rearrange("p a f -> p (a f)"), wcat[m].rearrange("p a f -> p (a f)"))],
                  reads=[B_wcat], writes=[wb])
            return wt, wb

        def fm_mm(pt, wt, rhs_tile, nk):
            def f(e):
                ins = None
                for kc in range(nk):
                    ins = e.matmul(pt[:], lhsT=wt[:, kc, :], rhs=rhs_tile[:, kc, :], start=(kc == 0), stop=(kc == nk - 1))
                return ins
            return f

        tiles = [(x_p, V_p, B_Vp, ys_p, B_ysp, y_p, t) for t in range(SP // 512)] + \
                [(x_s, V_s, B_Vs, ys_s, B_yss, y_s, t) for t in range(OWN // 512)]
        if "C" not in phases:
            tiles = []
        if "1" in phases:
            tiles = tiles[:2]

        def c_front_act(tile):
            (xsrc, Vd, BV, ysd, Bys, ydst, t) = tile
            for k in range(4):
                xk, B_xk = XK.next()
                S.dma("sp", B_xk, [(xk[:], xsrc[t * 512 + k * 128:t * 512 + (k + 1) * 128, :])], writes=[B_xk])
                S.op("act", lambda e, k=k, xk=xk: e.activation(out=junk[:], in_=xk[:], func=AF.Square,
                                                               accum_out=ssF[:, k:k + 1]),
                     reads=[B_xk], writes=[B_junk, B_ssF])
                S.op("dve", lambda e, k=k: e.tensor_copy(out=rstdF[:, k:k + 1], in_=ssF[:, k:k + 1]),
                     reads=[B_ssF], writes=[B_rstdF])
                rsq(rstdF[:, k:k + 1], B_rstdF)
                S.op("act", lambda e, k=k, xk=xk: e.activation(out=hbF[:, k, :], in_=xk[:], func=AF.Copy,
                                                               scale=rstdF[:, k:k + 1]),
                     reads=[B_xk, B_rstdF], writes=[B_hbF])

        def c_front_pe():
            transpose_hT(hbF, B_hbF, hT1, B_hT1)

        def c_gelu(tile):
            (xsrc, Vd, BV, ysd, Bys, ydst, t) = tile
            rows = slice(t * 512, (t + 1) * 512)
            S.dma("sp", B_yt, [(yt[:], ysd[rows, :].rearrange("(k p) c -> p k c", p=128))], reads=[Bys], writes=[B_yt])
            for cp in range(2):
                pt, pb = PB.next()

                def trz2(e, pt=pt, cp=cp):
                    ins = None
                    for c2 in range(2):
                        for k in range(4):
                            cc = cp * 2 + c2
                            ins = e.transpose(pt[:, (c2 * 4 + k) * 128:(c2 * 4 + k + 1) * 128],
                                              yt[:, k, cc * 128:(cc + 1) * 128], identb[:])
                    return ins
                S.op("pe", trz2, reads=[B_yt, B_identb], writes=[pb])
                for hh in range(2):
                    pv = pt[:, hh * 512:(hh + 1) * 512]
                    S.op("act", lambda e, pv=pv: e.activation(out=zp[:], in_=pv, func=AF.Square), reads=[pb], writes=[B_zp])
                    S.op("dve", lambda e: e.tensor_scalar(out=zp[:], in0=zp[:], scalar1=0.044715, scalar2=1.0,
                                                          op0=ALU.mult, op1=ALU.add), reads=[B_zp], writes=[B_zp])
                    S.op("dve", lambda e, pv=pv: e.tensor_tensor(out=zq[:], in0=pv, in1=zp[:], op=ALU.mult),
                         reads=[pb, B_zp], writes=[B_zq])
                    S.op("act", lambda e: e.activation(out=zp[:], in_=zq[:], func=AF.Sigmoid, scale=GELU_C),
                         reads=[B_zq], writes=[B_zp])
                    S.op("dve", lambda e, pv=pv, cp=cp, hh=hh: e.tensor_tensor(
                        out=zT[:, cp * 2 + hh, :], in0=pv, in1=zp[:], op=ALU.mult),
                        reads=[pb, B_zp], writes=[B_zT])

        def c_gates():
            for mc in range(16):
                wt, wb = load_w(mc)
                pt, pb = PF.next()
                S.op("pe", fm_mm(pt, wt, hT1, 8), reads=[wb, B_hT1], writes=[pb])
                S.op("act", lambda e, pt=pt, mc=mc: e.activation(out=sg[:, mc, :], in_=pt[:], func=AF.Sigmoid),
                     reads=[pb], writes=[B_sg])
        def c_branch_glu(tile):
            (xsrc, Vd, BV, ysd, Bys, ydst, t) = tile
            rows = slice(t * 512, (t + 1) * 512)
            S.dma("sp", B_vt, [(vt[:], Vd[:, rows].rearrange("(a p) t -> p a t", p=128))], reads=[BV], writes=[B_vt])
            for mc in range(8):
                pt, pb = PF.next()

                def bra(e, pt=pt, mc=mc):
                    ins = None
                    for kc in range(8):
                        ins = e.matmul(pt[:], lhsT=wfp[:, kc, mc * 128:(mc + 1) * 128], rhs=vt[:, kc, :],
                                       start=(kc == 0), stop=(kc == 7))
                    return ins
                S.op("pe", bra, reads=[B_wfp, B_vt], writes=[pb])
                S.op("dve", lambda e, pt=pt, mc=mc: e.tensor_tensor(out=mg[:, mc, :], in0=pt[:], in1=sg[:, mc, :], op=ALU.mult),
                     reads=[pb, B_sg], writes=[B_mg])
            for mc in range(8):
                ptv, pbv = PF.next()
                ptg, pbg = PF.next()

                def glu(e, ptv=ptv, ptg=ptg, mc=mc):
                    ins = None
                    for kc in range(4):
                        e.matmul(ptv[:], lhsT=wv[:, kc, mc * 128:(mc + 1) * 128], rhs=zT[:, kc, :], start=(kc == 0), stop=(kc == 3))
                    for kc in range(4):
                        ins = e.matmul(ptg[:], lhsT=wgg[:, kc, mc * 128:(mc + 1) * 128], rhs=zT[:, kc, :], start=(kc == 0), stop=(kc == 3))
                    return ins
                S.op("pe", glu, reads=[B_wv, B_wgg, B_zT], writes=[pbv, pbg])
                s_, B_s = SGT.next()
                t_, B_t = TM.next()
                S.op("act", lambda e, ptg=ptg, s_=s_: e.activation(out=s_[:], in_=ptg[:], func=AF.Sigmoid),
                     reads=[pbg], writes=[B_s])
                S.op("dve", lambda e, ptv=ptv, s_=s_, t_=t_: e.tensor_tensor(out=t_[:], in0=ptv[:], in1=s_[:], op=ALU.mult),
                     reads=[pbv, B_s], writes=[B_t])
                S.op("dve", lambda e, t_=t_, mc=mc: e.tensor_tensor(out=t_[:], in0=t_[:], in1=sg[:, 8 + mc, :], op=ALU.mult),
                     reads=[B_t, B_sg], writes=[B_t])
                S.op("pool", lambda e, t_=t_, mc=mc: e.tensor_tensor(out=mg[:, mc, :], in0=t_[:], in1=mg[:, mc, :], op=ALU.add),
                     reads=[B_t, B_mg], writes=[B_mg])
        def c_wout(tile):
            (xsrc, Vd, BV, ysd, Bys, ydst, t) = tile
            rows = slice(t * 512, (t + 1) * 512)
            S.dma("sp", B_xt, [(xt[:], xsrc[rows, :].rearrange("(k p) d -> p k d", p=128))], writes=[B_xt])
            for k in range(4):
                pts = []
                for hf in range(2):
                    pt, pb = PF.next()

                    def wom(e, pt=pt, k=k, hf=hf):
                        ins = None
                        for dc in range(8):
                            ins = e.matmul(pt[:], lhsT=mg[:, dc, k * 128:(k + 1) * 128], rhs=wo_[:, dc, hf * 512:(hf + 1) * 512],
                                           start=(dc == 0), stop=(dc == 7))
                        return ins
                    S.op("pe", wom, reads=[B_mg, B_wo], writes=[pb])
                    S.op("act", lambda e, pt=pt, k=k, hf=hf: e.activation(out=junk[:, 0:512], in_=pt[:], func=AF.Square,
                                                                          accum_out=ss[:, k * 2 + hf:k * 2 + hf + 1]),
                         reads=[pb], writes=[B_junk, B_ss])
                    pts.append((pt, pb))
                S.op("dve", lambda e, k=k: e.tensor_tensor(out=rstd[:, k:k + 1], in0=ss[:, 2 * k:2 * k + 1], in1=ss[:, 2 * k + 1:2 * k + 2], op=ALU.add),
                     reads=[B_ss], writes=[B_rstd])
                rsq(rstd[:, k:k + 1], B_rstd)
                for hf in range(2):
                    pt, pb = pts[hf]
                    t_, B_t = TM.next()
                    S.op("dve", lambda e, pt=pt, k=k, hf=hf, t_=t_: e.scalar_tensor_tensor(
                        out=t_[:], in0=pt[:], scalar=rstd[:, k:k + 1], in1=g2t[:, hf * 512:(hf + 1) * 512], op0=ALU.mult, op1=ALU.mult),
                        reads=[pb, B_rstd, B_g2t], writes=[B_t])
                    S.op("pool", lambda e, k=k, hf=hf, t_=t_: e.tensor_tensor(
                        out=xt[:, k, hf * 512:(hf + 1) * 512], in0=xt[:, k, hf * 512:(hf + 1) * 512], in1=t_[:], op=ALU.add),
                        reads=[B_t, B_xt], writes=[B_xt])
        def c_rms3_act():
            for k in range(4):
                S.op("dve", lambda e, k=k: e.scalar_tensor_tensor(out=junk[:], in0=xt[:, k, :], scalar=1.0, in1=xt[:, k, :],
                                                                  op0=ALU.mult, op1=ALU.mult, accum_out=ss[:, k:k + 1]),
                     reads=[B_xt], writes=[B_junk, B_ss])
            S.op("dve", lambda e: e.tensor_copy(out=rstd[:, 0:4], in_=ss[:, 0:4]), reads=[B_ss], writes=[B_rstd])
            rsq(rstd[:, 0:4], B_rstd)
            for k in range(4):
                S.op("dve", lambda e, k=k: e.tensor_scalar(out=hbH[:, k, :], in0=xt[:, k, :], scalar1=rstd[:, k:k + 1],
                                                           scalar2=None, op0=ALU.mult),
                     reads=[B_xt, B_rstd], writes=[B_hbH])

        def c_hT3_pe():
            transpose_hT(hbH, B_hbH, hT3, B_hT3)

        def c_ffn_up():
            for fc in range(NFC):
                wtg, wbg = load_w(16 + fc)
                wtu, wbu = load_w(38 + fc)
                ptg, pbg = PF.next()
                ptu, pbu = PF.next()
                S.op("pe", fm_mm(ptg, wtg, hT3, 8), reads=[wbg, B_hT3], writes=[pbg])
                S.op("pe", fm_mm(ptu, wtu, hT3, 8), reads=[wbu, B_hT3], writes=[pbu])
                s_, B_s = SGT.next()
                S.op("act", lambda e, ptg=ptg, s_=s_: e.activation(out=s_[:], in_=ptg[:], func=AF.Silu), reads=[pbg], writes=[B_s])
                S.op("dve", lambda e, ptu=ptu, s_=s_, fc=fc: e.tensor_tensor(out=af[:, fc, :], in0=ptu[:], in1=s_[:], op=ALU.mult),
                     reads=[pbu, B_s], writes=[B_af])

        def c_down_mm():
            banks = [PF.next() for _ in range(6)] + [(psb[0][0][:].bitcast(F32), psb[0][1]), (psb[1][0][:].bitcast(F32), psb[1][1])]
            grp = []
            for k in range(4):
                for hf in range(2):
                    bt, bb = banks[k * 2 + hf]
                    grp.append((k, hf, bt, bb))
            for fc in range(NFC):
                wd, wdb = WDS.next()
                S.dma("sp", wdb, [(wd[:], wd_sc[fc * 128:(fc + 1) * 128, :])], reads=[B_wd], writes=[wdb])

                def dmm(e, grp=grp, wd=wd, fc=fc):
                    ins = None
                    for (k, hf, pt, pb) in grp:
                        ins = e.matmul(pt[:, 0:512], lhsT=af[:, fc, k * 128:(k + 1) * 128], rhs=wd[:, hf * 512:(hf + 1) * 512],
                                       start=(fc == 0), stop=(fc == NFC - 1))
                    return ins
                S.op("pe", dmm, reads=[wdb, B_af], writes=[g_[3] for g_ in grp])
            return grp

        def c_down_epi(tile, grp):
            (xsrc, Vd, BV, ysd, Bys, ydst, t) = tile
            for k in range(4):
                for (k_, hf, pt, pb) in grp:
                    if k_ != k:
                        continue
                    S.op("act", lambda e, pt=pt, k=k, hf=hf: e.activation(out=junk[:, 0:512], in_=pt[:, 0:512], func=AF.Square,
                                                                          accum_out=ss[:, k * 2 + hf:k * 2 + hf + 1]),
                         reads=[pb], writes=[B_junk, B_ss])
                S.op("dve", lambda e, k=k: e.tensor_tensor(out=rstd[:, k:k + 1], in0=ss[:, 2 * k:2 * k + 1], in1=ss[:, 2 * k + 1:2 * k + 2], op=ALU.add),
                     reads=[B_ss], writes=[B_rstd])
                rsq(rstd[:, k:k + 1], B_rstd)
                o_, B_o = OB.next()
                for hf in range(2):
                    pt, pb = [(g_[2], g_[3]) for g_ in grp if g_[0] == k and g_[1] == hf][0]
                    t_, B_t = TM.next()
                    S.op("dve", lambda e, pt=pt, k=k, hf=hf, t_=t_: e.scalar_tensor_tensor(
                        out=t_[:], in0=pt[:, 0:512], scalar=rstd[:, k:k + 1], in1=g4t[:, hf * 512:(hf + 1) * 512], op0=ALU.mult, op1=ALU.mult),
                        reads=[pb, B_rstd, B_g4t], writes=[B_t])
                    S.op("pool", lambda e, k=k, hf=hf, t_=t_, o_=o_: e.tensor_tensor(
                        out=o_[:, hf * 512:(hf + 1) * 512], in0=xt[:, k, hf * 512:(hf + 1) * 512], in1=t_[:], op=ALU.add),
                        reads=[B_t, B_xt], writes=[B_o])
                S.dma("pool", B_o, [(ydst[t * 512 + k * 128:t * 512 + (k + 1) * 128, :], o_[:])],
                      reads=[B_o], writes=[B_dummy_out], final=True)

        if tiles:
            c_front_act(tiles[0])
            c_front_pe()
            c_gelu(tiles[0])
            c_gates()
        for ti, tile in enumerate(tiles):
            nxt = tiles[ti + 1] if ti + 1 < len(tiles) else None
            c_branch_glu(tile)
            if nxt is not None:
                c_front_act(nxt)
            c_wout(tile)
            if nxt is not None:
                c_front_pe()
            c_rms3_act()
            if nxt is not None:
                c_gelu(nxt)
                c_gates()
            c_hT3_pe()
            c_ffn_up()
            g_all = c_down_mm()
            c_down_epi(tile, g_all)
    ph_c.close()
    S.finish()
    es.close()
    return nc


def _consts():
    bf = ml_dtypes.bfloat16
    c = {}
    c["c_identb"] = np.eye(128, dtype=np.float32).astype(bf)
    c["c_identf"] = np.eye(128, dtype=np.float32)
    n1 = np.arange(128)[:, None].astype(np.float64)
    k1 = np.arange(128)[None, :].astype(np.float64)
    ang = TWO_PI * n1 * k1 / 128.0
    f1 = np.zeros((128, 2, 128), np.float64)
    for kh in range(2):
        f1[:, kh, 0:64] = np.cos(ang[:, kh * 64:(kh + 1) * 64])
        f1[:, kh, 64:128] = np.sin(ang[:, kh * 64:(kh + 1) * 64])
    c["c_f1"] = f1.astype(np.float32).astype(bf)
    n2 = np.arange(64)[:, None, None].astype(np.float64)
    kk = (np.arange(128)[None, :, None] + 128 * np.arange(64)[None, None, :]).astype(np.float64)
    ph = -TWO_PI * n2 * kk / SP
    gr, gi = np.cos(ph) / math.sqrt(SP), np.sin(ph) / math.sqrt(SP)
    c["c_gp"] = np.concatenate([gr, gi, -gr], axis=2).astype(np.float32).astype(bf)
    ch = np.arange(128)[:, None].astype(np.float64)
    ch2 = np.arange(128)[None, :].astype(np.float64)
    a2 = TWO_PI * ch * ch2 / 128.0
    c["c_cs"] = (np.concatenate([np.cos(a2), np.sin(a2)], axis=1) / math.sqrt(128.0)).astype(np.float32)
    c["c_kv17"] = np.tile(np.arange(-8, 9, dtype=np.float32)[None, :], (128, 1))
    c["c_kv129"] = np.tile(np.arange(0, 129, dtype=np.float32)[None, :], (128, 1))
    ii = (np.arange(128) // 16)[:, None]
    jj = (np.arange(128) // 16)[None, :]
    c["c_mf"] = (ii <= jj).astype(np.float32)
    c["c_mb"] = (ii >= jj).astype(np.float32)
    return c


def _core_consts(q):
    bf = ml_dtypes.bfloat16
    c = {}
    s = OWN * q
    n2 = np.arange(128)[:, None, None].astype(np.float64)
    kk = (np.arange(128)[None, :, None] + 128 * (16 * q + np.arange(16))[None, None, :]).astype(np.float64)
    ph = -TWO_PI * ((n2 + s) * kk % SS) / SS
    gr, gi = np.cos(ph) / math.sqrt(SS), np.sin(ph) / math.sqrt(SS)
    c["c_gs"] = np.concatenate([gr, gi, -gr], axis=2).astype(np.float32).astype(bf)
    mk = np.ones((128, 16), np.float32)
    for j in range(16):
        sf = (2 + j) % 16
        sbk = 15 - j
        if sf == (16 - 2 * q) % 16:
            mk[0:64, j] = 0.0
        if sbk == 15 - 2 * q:
            mk[64:128, j] = 0.0
    c["c_mask"] = mk
    return c


_NC_CACHE = {}


def kernel(x_prompt, x_sample, norm_mix_pre, norm_mix_post, norm_ffn_pre, norm_ffn_post, w_in,
           w_fnet_out, lam_re, lam_im, log_dt, b_re, b_im, c_re, c_im, d_skip, w_glu_val,
           w_glu_gate, w_out, w_ffn_gate, w_ffn_up, w_ffn_down):
    f32 = np.float32
    if "nc" not in _NC_CACHE:
        _NC_CACHE["nc"] = build_program()
    nc = _NC_CACHE["nc"]
    shared = {
        "gains": np.ascontiguousarray(np.stack([np.asarray(norm_mix_pre, f32)[0], np.asarray(norm_mix_post, f32)[0],
                                                np.asarray(norm_ffn_pre, f32)[0], np.asarray(norm_ffn_post, f32)[0]])),
        "w_in": np.ascontiguousarray(np.asarray(w_in, f32)[0]),
        "w_fn": np.ascontiguousarray(np.asarray(w_fnet_out, f32)[0]),
        "lam_re": np.ascontiguousarray(np.asarray(lam_re, f32)[0]),
        "lam_im": np.ascontiguousarray(np.asarray(lam_im, f32)[0]),
        "log_dt": np.ascontiguousarray(np.asarray(log_dt, f32)[0]),
        "b_re": np.ascontiguousarray(np.asarray(b_re, f32)[0]),
        "b_im": np.ascontiguousarray(np.asarray(b_im, f32)[0]),
        "c_re": np.ascontiguousarray(np.asarray(c_re, f32)[0]),
        "c_im": np.ascontiguousarray(np.asarray(c_im, f32)[0]),
        "d_skip": np.ascontiguousarray(np.asarray(d_skip, f32)[0]),
        "w_gv": np.ascontiguousarray(np.asarray(w_glu_val, f32)[0]),
        "w_gg": np.ascontiguousarray(np.asarray(w_glu_gate, f32)[0]),
        "w_o": np.ascontiguousarray(np.asarray(w_out, f32)[0]),
        "w_fg": np.ascontiguousarray(np.asarray(w_ffn_gate, f32)[0]),
        "w_fu": np.ascontiguousarray(np.asarray(w_ffn_up, f32)[0]),
        "w_fd": np.ascontiguousarray(np.asarray(w_ffn_down, f32)[0]),
    }
    shared.update(_consts())
    xp = np.asarray(x_prompt, f32)
    xs = np.asarray(x_sample, f32)[0]
    in_maps = []
    for q in range(8):
        m = dict(shared)
        m["x_p"] = np.ascontiguousarray(xp[q])
        m["x_s"] = np.ascontiguousarray(np.roll(xs, -OWN * q, axis=0))
        m.update(_core_consts(q))
        in_maps.append(m)
    res = run_bass_kernel_spmd(nc, in_maps, core_ids=list(range(8)))
    yp = np.stack([np.asarray(res.results[q]["y_p"], f32) for q in range(8)], axis=0)
    ysm = np.concatenate([np.asarray(res.results[q]["y_s"], f32) for q in range(8)], axis=0)[None]
    return (yp, ysm)
```
